# Optimizing a Trainium2 kernel written in Bass

```python
import math
import jax, jax.numpy as jnp
from jax import lax
import numpy as np

D_MODEL = 1024
BATCH = 4
SEQ = 8192
DEPTH = 1

N_HEADS = 8
HEAD_DIM = 128
ATTN_WIDTH = N_HEADS * HEAD_DIM
CONV_WIDTH = D_MODEL
CONV_K = 3
D_FF = 2816
PLE_DIM = 256
Q_BLOCK = 128
NORM_EPS = 1e-6
MIX_COLS = 3 * CONV_WIDTH + 3 * ATTN_WIDTH + 2 * D_MODEL

kernel_name = "hybrid_shortconv_stickbreaking_macaron_block"


def rms_norm(x, g):
    xf = x.astype(jnp.float32)
    y = xf * lax.rsqrt(jnp.mean(xf * xf, axis=-1, keepdims=True) + NORM_EPS)
    return (y * g.astype(jnp.float32)).astype(x.dtype)


def swiglu(x, w_in, w_out):
    gate, up = jnp.split(x @ w_in, 2, axis=-1)
    return (jax.nn.silu(gate) * up) @ w_out


def causal_depthwise_conv(x, w):
    return lax.conv_general_dilated(
        x, w[:, None, :].astype(x.dtype), window_strides=(1,),
        padding=[(CONV_K - 1, 0)], dimension_numbers=('NWC', 'WIO', 'NWC'),
        feature_group_count=x.shape[-1])


def stick_breaking_attention(q, k, v):
    b, h, s, d = q.shape
    nblk = s // Q_BLOCK
    scale = 1.0 / math.sqrt(d)
    k_pos = jnp.arange(s)
    vf = v.astype(jnp.float32)
    qb = q.reshape(b, h, nblk, Q_BLOCK, d).transpose(2, 0, 1, 3, 4)

    def block(args):
        q_blk, blk_idx = args
        q_pos = blk_idx * Q_BLOCK + jnp.arange(Q_BLOCK)
        z = jnp.einsum('bhqd,bhkd->bhqk', q_blk, k,
                       preferred_element_type=jnp.float32) * scale
        mask = k_pos[None, :] < q_pos[:, None]
        log_1m_beta = jnp.where(mask, jax.nn.log_sigmoid(-z), 0.0)
        tail = lax.cumsum(log_1m_beta, axis=3, reverse=True) - log_1m_beta
        a = jnp.where(mask, jnp.exp(jax.nn.log_sigmoid(z) + tail), 0.0)
        return jnp.einsum('bhqk,bhkd->bhqd', a, vf).astype(q.dtype)

    out = lax.map(block, (qb, jnp.arange(nblk)))
    return out.transpose(1, 2, 0, 3, 4).reshape(b, h, s, d)


def mix_split_points():
    widths = [CONV_WIDTH, CONV_WIDTH, CONV_WIDTH, ATTN_WIDTH, ATTN_WIDTH, ATTN_WIDTH, D_MODEL, D_MODEL]
    pts, acc = [], 0
    for w in widths[:-1]:
        acc += w
        pts.append(acc)
    return pts


def setup_inputs(seed: int = 0) -> dict:
    key = jax.random.key(seed)
    ks = jax.random.split(key, 20)

    def w(k, shape, fan_in):
        return jax.random.normal(k, shape, jnp.float32) * (fan_in ** -0.5)

    def gain(k, shape):
        return 1.0 + 0.01 * jax.random.normal(k, shape, jnp.float32)

    return {
        "x": jax.random.normal(ks[0], (BATCH, SEQ, D_MODEL), jnp.float32),
        "p": jax.random.normal(ks[1], (DEPTH, BATCH, SEQ, PLE_DIM), jnp.float32),
        "ffn1_norm": gain(ks[2], (DEPTH, D_MODEL)),
        "ffn1_w_in": w(ks[3], (DEPTH, D_MODEL, 2 * D_FF), D_MODEL),
        "ffn1_w_out": w(ks[4], (DEPTH, D_FF, D_MODEL), D_FF),
        "mix_norm": gain(ks[5], (DEPTH, D_MODEL)),
        "w_mix_in": w(ks[6], (DEPTH, D_MODEL, MIX_COLS), D_MODEL),
        "conv_w": w(ks[7], (DEPTH, CONV_K, CONV_WIDTH), CONV_K),
        "w_conv_out": w(ks[8], (DEPTH, CONV_WIDTH, D_MODEL), CONV_WIDTH),
        "w_attn_out": w(ks[9], (DEPTH, ATTN_WIDTH, D_MODEL), ATTN_WIDTH),
        "w_mix_out": w(ks[10], (DEPTH, D_MODEL, D_MODEL), D_MODEL),
        "ffn2_norm": gain(ks[11], (DEPTH, D_MODEL)),
        "ffn2_w_in": w(ks[12], (DEPTH, D_MODEL, 2 * D_FF), D_MODEL),
        "ffn2_w_out": w(ks[13], (DEPTH, D_FF, D_MODEL), D_FF),
        "ple_norm": gain(ks[14], (DEPTH, D_MODEL)),
        "w_ple_gate": w(ks[15], (DEPTH, D_MODEL, D_MODEL), D_MODEL),
        "w_ple_proj": w(ks[16], (DEPTH, PLE_DIM, D_MODEL), PLE_DIM),
        "final_norm": gain(ks[17], (D_MODEL,)),
    }


def reference(x, p, ffn1_norm, ffn1_w_in, ffn1_w_out, mix_norm, w_mix_in, conv_w,
              w_conv_out, w_attn_out, w_mix_out, ffn2_norm, ffn2_w_in, ffn2_w_out,
              ple_norm, w_ple_gate, w_ple_proj, final_norm):
    b, s, _ = x.shape
    splits = mix_split_points()
    h = x
    for i in range(DEPTH):
        h = h + 0.5 * swiglu(rms_norm(h, ffn1_norm[i]), ffn1_w_in[i], ffn1_w_out[i])

        u = rms_norm(h, mix_norm[i])
        c_b, c_c, c_x, q, k, v, g_conv, g_attn = jnp.split(u @ w_mix_in[i], splits, axis=-1)

        y_conv = (c_b * causal_depthwise_conv(c_c * c_x, conv_w[i])) @ w_conv_out[i]

        def heads(t):
            return t.reshape(b, s, N_HEADS, HEAD_DIM).transpose(0, 2, 1, 3)
        o = stick_breaking_attention(heads(q), heads(k), heads(v))
        y_attn = o.transpose(0, 2, 1, 3).reshape(b, s, ATTN_WIDTH) @ w_attn_out[i]

        merged = jax.nn.sigmoid(g_conv) * y_conv + jax.nn.sigmoid(g_attn) * y_attn
        h = h + merged @ w_mix_out[i]

        h = h + 0.5 * swiglu(rms_norm(h, ffn2_norm[i]), ffn2_w_in[i], ffn2_w_out[i])

        ple_gate = jax.nn.sigmoid(rms_norm(h, ple_norm[i]) @ w_ple_gate[i])
        h = h + ple_gate * (p[i] @ w_ple_proj[i])

    return rms_norm(h, final_norm)
```

```python
import math
from contextlib import ExitStack

import numpy as np
import concourse.bass as bass
import concourse.mybir as mybir
from concourse.bass_utils import run_bass_kernel_spmd

F32 = mybir.dt.float32
BF16 = mybir.dt.bfloat16
AF = mybir.ActivationFunctionType
ALU = mybir.AluOpType

D = 1024
H = 8
DFF = 2816
PLE = 256
KC = 8
FC = 22
EPS = 1e-6
NEGBIG = -30000.0
RING = 3
SLOT_ELEMS = 4096


class Buf:
    __slots__ = ("name", "w", "r", "dsem")

    def __init__(self, name):
        self.name = name
        self.w = None
        self.r = {}
        self.dsem = None


class DSem:
    __slots__ = ("sem", "cnt")

    def __init__(self, sem):
        self.sem = sem
        self.cnt = 0


class Eng:
    def __init__(self, handle, sem, selfwait):
        self.h = handle
        self.sem = sem
        self.cnt = 0
        self.waited = {}
        self.selfwait = selfwait

    def wait(self, ev):
        sem, val = ev
        if sem is self.sem and not self.selfwait:
            return
        if self.waited.get(sem, 0) >= val:
            return
        self.h.wait_ge(sem, val)
        self.waited[sem] = val


def _deps(reads, writes):
    deps = {}
    for b in reads:
        if b.w is not None:
            s, v = b.w
            if deps.get(s, 0) < v:
                deps[s] = v
    for b in writes:
        if b.w is not None:
            s, v = b.w
            if deps.get(s, 0) < v:
                deps[s] = v
        for s, v in b.r.items():
            if deps.get(s, 0) < v:
                deps[s] = v
    return deps


def _record(ev, reads, writes):
    s, v = ev
    for b in reads:
        if b.r.get(s, 0) < v:
            b.r[s] = v
    for b in writes:
        b.w = ev
        b.r = {}


def op(eng, fn, reads=(), writes=()):
    for s, v in _deps(reads, writes).items():
        eng.wait((s, v))
    ins = fn()
    ins.then_inc(eng.sem, 1)
    eng.cnt += 1
    ev = (eng.sem, eng.cnt)
    _record(ev, reads, writes)
    return ev


def op_group(eng, fns, reads=(), writes=()):
    for s, v in _deps(reads, writes).items():
        eng.wait((s, v))
    ins = None
    for fn in fns:
        ins = fn()
    ins.then_inc(eng.sem, 1)
    eng.cnt += 1
    ev = (eng.sem, eng.cnt)
    _record(ev, reads, writes)
    return ev


def dma(q, ds, out_ap, in_ap, reads=(), writes=(), extra=()):
    deps = _deps(reads, writes)
    for s_, v_ in extra:
        if deps.get(s_, 0) < v_:
            deps[s_] = v_
    if ds.cnt:
        if deps.get(ds.sem, 0) < ds.cnt:
            deps[ds.sem] = ds.cnt
    for s, v in deps.items():
        q.wait((s, v))
    q.h.dma_start(out=out_ap, in_=in_ap).then_inc(ds.sem, 16)
    ds.cnt += 16
    ev = (ds.sem, ds.cnt)
    _record(ev, reads, writes)
    return ev


def build(S):
    NCH = S // 512
    NO = NCH // 2
    NB = S // 128
    SO = S // 2
    nc = bass.Bass("TRN2", target_bir_lowering=False)

    def din(name, shape, dt=F32):
        return nc.dram_tensor(name, list(shape), dt, kind="ExternalInput").ap()

    xT = din("xT", [D, S])
    pT = din("pT", [PLE, SO])
    W = {
        "in1": din("w_in1", [D, 2 * DFF]),
        "out1": din("w_out1", [DFF, D]),
        "mix": din("w_mix", [D, 8 * D]),
        "co": din("w_co", [D, D]),
        "ao": din("w_ao", [D, D]),
        "mo": din("w_mo", [D, D]),
        "in2": din("w_in2", [D, 2 * DFF]),
        "out2": din("w_out2", [DFF, D]),
        "pg": din("w_pg", [D, D]),
        "pp": din("w_pp", [PLE, D]),
    }
    vecs_d = din("vecs", [128, 64])
    consts_d = din("consts", [128, 640])
    outT = nc.dram_tensor("outT", [D, SO], F32, kind="ExternalOutput").ap()

    kT_s = nc.dram_tensor("kT_s", [H, 128, S], BF16).ap()
    qT_s = nc.dram_tensor("qT_s", [H, 128, SO], BF16).ap()
    v_s = nc.dram_tensor("v_s", [128, NB, D], BF16).ap()
    h1_s = nc.dram_tensor("h1_s", [128, KC, SO], F32).ap()
    oT_s = nc.dram_tensor("oT_s", [128, H, SO], BF16).ap()

    slabs = {}

    def add_slabs(prefix, w_ap, col0, ncols, cw, nm):
        K = w_ap.shape[0]
        kc = K // 128
        ns = ncols // cw
        wb = nc.dram_tensor("wb_" + nm, [ns, 128, kc, cw], BF16).ap()
        src = w_ap.rearrange("(k p) n -> p k n", p=128)
        for j in range(ns):
            slabs[(prefix, j)] = (src[:, :, col0 + j * cw: col0 + (j + 1) * cw], wb[j], kc, cw)

    add_slabs("in1g", W["in1"], 0, DFF, 256, "in1g")
    add_slabs("in1u", W["in1"], DFF, DFF, 256, "in1u")
    add_slabs("out1", W["out1"], 0, D, 128, "out1")
    add_slabs("mix", W["mix"], 0, 8 * D, 512, "mix")
    add_slabs("co", W["co"], 0, D, 512, "co")
    add_slabs("ao", W["ao"], 0, D, 512, "ao")
    add_slabs("mo", W["mo"], 0, D, 512, "mo")
    add_slabs("in2g", W["in2"], 0, DFF, 256, "in2g")
    add_slabs("in2u", W["in2"], DFF, DFF, 256, "in2u")
    add_slabs("out2", W["out2"], 0, D, 128, "out2")
    add_slabs("pg", W["pg"], 0, D, 512, "pg")
    add_slabs("pp", W["pp"], 0, D, 1024, "pp")

    plan = []
    for i in range(NO):
        for j in range(11):
            plan += [("in1g", j), ("in1u", j)]
        plan += [("out1", dt) for dt in range(8)]
        plan += [("mix", s) for s in (6, 7, 8, 9, 10, 11)]
    for i in range(NO):
        plan += [("mix", 2), ("mix", 4), ("mix", 3), ("mix", 5)]
        plan += [("mix", 0), ("mix", 1)]
        plan += [("mix", 12), ("co", 0), ("mix", 13), ("co", 1)]
        plan += [("mix", 14), ("ao", 0), ("mix", 15), ("ao", 1)]
        plan += [("mo", 0), ("mo", 1)]
        for j in range(11):
            plan += [("in2g", j), ("in2u", j)]
        plan += [("out2", dt) for dt in range(8)]
        plan += [("pp", 0), ("pg", 0), ("pg", 1)]

    with ExitStack() as g:
        def sem(name):
            return g.enter_context(nc.semaphore(name))

        PE = Eng(nc.tensor, sem("s_pe"), False)
        ACT = Eng(nc.scalar, sem("s_act"), True)
        DVE = Eng(nc.vector, sem("s_dve"), True)
        POOL = Eng(nc.gpsimd, sem("s_pool"), True)
        SP = Eng(nc.sync, sem("s_sp"), False)
        engines = [PE, ACT, DVE, POOL, SP]
        dsems = []

        def new_dsem(name):
            ds = DSem(sem(name))
            dsems.append(ds)
            return ds

        def barrier():
            for e in engines:
                for f in engines:
                    if f is not e and f.cnt:
                        e.wait((f.sem, f.cnt))
                for ds in dsems:
                    if ds.cnt:
                        e.wait((ds.sem, ds.cnt))

        def sb(es, name, shape, dt, with_dsem=False):
            t = es.enter_context(nc.sbuf_tensor("sb_" + name, list(shape), dt))
            b = Buf(name)
            if with_dsem:
                b.dsem = new_dsem("d_" + name)
            return t, b

        vecs, vecs_b = sb(g, "vecs", [128, 64], F32, True)
        cst_f, cst_fb = sb(g, "cst_f", [128, 640], F32, True)
        cst_h, cst_hb = sb(g, "cst_h", [128, 640], BF16)
        epsT, eps_b = sb(g, "epsT", [128, 1], F32)
        halo, halo_b = sb(g, "halo", [128, NO, KC, 2], BF16)
        ring_t, ring_b = [], []
        for r in range(RING):
            t, b = sb(g, f"ring{r}", [128, SLOT_ELEMS], BF16, True)
            ring_t.append(t)
            ring_b.append(b)
        PS, PSb = [], []
        for r in range(8):
            t = g.enter_context(nc.psum_tensor(f"ps{r}", [128, 512], F32))
            PS.append(t)
            PSb.append(Buf(f"ps{r}"))

        dma(SP, vecs_b.dsem, vecs[:], vecs_d, writes=[vecs_b])
        dma(SP, cst_fb.dsem, cst_f[:], consts_d, writes=[cst_fb])
        op(DVE, lambda: nc.vector.tensor_copy(out=cst_h[:, 0:256], in_=cst_f[:, 0:256]), [cst_fb], [cst_hb])
        op(DVE, lambda: nc.vector.memset(cst_h[:, 256:384], 1.0), [], [cst_hb])
        op(DVE, lambda: nc.vector.tensor_copy(out=cst_h[:, 384:640], in_=cst_f[:, 384:640]), [cst_fb], [cst_hb])
        op(DVE, lambda: nc.vector.memset(epsT[:], EPS), [], [eps_b])
        oneT, one_b = sb(g, "oneT", [128, 1], F32)
        op(DVE, lambda: nc.vector.memset(oneT[:], 1.0), [], [one_b])
        M_incl = cst_h[:, 0:128]
        M_lt = cst_h[:, 128:256]
        ones_h = cst_h[:, 256:384]
        negm_h = cst_h[:, 384:512]
        ident_h = cst_h[:, 512:640]
        mask_f = cst_f[:, 256:384]
        negb_f = cst_f[:, 384:512]

        def vcol(v, kc):
            return vecs[:, v * 8 + kc: v * 8 + kc + 1]

        cast_ds = [new_dsem(f"d_cast{r}") for r in range(4)]
        slab_buf = {}
        order = []
        for k in plan:
            if k not in slab_buf:
                slab_buf[k] = Buf("slab")
                order.append(k)
        def _is_a(k):
            return k[0] in ("in1g", "in1u", "out1") or (k[0] == "mix" and 6 <= k[1] <= 11)

        cast_n = [0]

        def emit_casts(keys):
            for k in keys:
                src, dst, kc, cw = slabs[k]
                dma(POOL, cast_ds[cast_n[0] % 4], dst, src, writes=[slab_buf[k]])
                cast_n[0] += 1

        emit_casts([k for k in order if _is_a(k)])
        late_casts = [k for k in order if not _is_a(k)]
        if NO == 1:
            emit_casts(late_casts)
            late_casts = []

        ring_state = {"idx": 0, "issued": 0}

        def ring_next(key, hold=0):
            idx = ring_state["idx"]
            assert plan[idx] == key, (idx, plan[idx], key)
            lim = min(len(plan), idx - hold + RING)
            while ring_state["issued"] < lim:
                j = ring_state["issued"]
                src, dst, kc, cw = slabs[plan[j]]
                sl = j % RING
                view = ring_t[sl][:, 0:kc * cw].rearrange("p (k c) -> p k c", k=kc)
                dma(SP, ring_b[sl].dsem, view, dst, reads=[slab_buf[plan[j]]], writes=[ring_b[sl]])
                ring_state["issued"] += 1
            assert ring_state["issued"] > idx
            ring_state["idx"] += 1
            src, dst, kc, cw = slabs[key]
            sl = idx % RING
            return ring_t[sl][:, 0:kc * cw].rearrange("p (k c) -> p k c", k=kc), ring_b[sl]

        ps_state = {"n": 0}

        def ps_next(lo=0, hi=8):
            n = ps_state["n"]
            ps_state["n"] += 1
            r = lo + n % (hi - lo)
            return PS[r], PSb[r]

        def mm_group(ps, psb, items, reads):
            n = len(items)
            fns = []
            for t, (l, r_) in enumerate(items):
                fns.append(lambda l=l, r_=r_, t=t: nc.tensor.matmul(
                    ps, lhsT=l, rhs=r_, start=(t == 0), stop=(t == n - 1), skip_group_check=True))
            return op_group(PE, fns, reads, [psb])

        def rmsnorm(x, xb, v, u, ub, Wd, tmps, rstd, rstd_b):
            sq, sqb = tmps["sq"]
            for kc in range(KC):
                op(ACT, lambda kc=kc: nc.scalar.activation(out=sq[:, kc, 0:Wd], in_=x[:, kc, 0:Wd], func=AF.Square),
                   [xb], [sqb])
            for hf in range(Wd // 512):
                ps, psb = ps_next()
                cs = slice(hf * 512, (hf + 1) * 512)
                mm_group(ps[:], psb, [(ones_h, sq[:, kc, cs]) for kc in range(KC)], [cst_hb, sqb])
                op(ACT, lambda ps=ps, cs=cs: nc.scalar.activation(out=rstd[:, cs], in_=ps[:], func=AF.Sqrt,
                                                                  bias=epsT[:], scale=1.0 / D),
                   [psb, eps_b], [rstd_b])
                op(DVE, lambda cs=cs: nc.vector.reciprocal(out=rstd[:, cs], in_=rstd[:, cs]), [rstd_b], [rstd_b])
            for kc in range(KC):
                op(DVE, lambda kc=kc: nc.vector.scalar_tensor_tensor(
                    out=u[:, kc, 0:Wd], in0=x[:, kc, 0:Wd], scalar=vcol(v, kc), in1=rstd[:, 0:Wd],
                    op0=ALU.mult, op1=ALU.mult), [xb, vecs_b, rstd_b], [ub])

        def ffn(u, ub, x, xb, hid, hidb, tmpl, kin_g, kin_u, kout, Wd):
            nh = Wd // 512
            tcount = 0
            for j in range(11):
                sg, sgb = ring_next((kin_g, j), 0)
                su, sub = ring_next((kin_u, j), 1)
                for ff in range(2):
                    f = 2 * j + ff
                    fs = slice(ff * 128, (ff + 1) * 128)
                    for hf in range(nh):
                        cs = slice(hf * 512, (hf + 1) * 512)
                        pg, pgb = ps_next()
                        pu, pub = ps_next()
                        mm_group(pg[:], pgb, [(sg[:, kc, fs], u[:, kc, cs]) for kc in range(KC)], [sgb, ub])
                        mm_group(pu[:], pub, [(su[:, kc, fs], u[:, kc, cs]) for kc in range(KC)], [sub, ub])
                        tt, ttb = tmpl[tcount % len(tmpl)]
                        tcount += 1
                        op(ACT, lambda tt=tt, pg=pg: nc.scalar.activation(out=tt[:], in_=pg[:], func=AF.Silu),
                           [pgb], [ttb])
                        op(DVE, lambda tt=tt, pu=pu, f=f, cs=cs: nc.vector.tensor_tensor(
                            out=hid[:, f, cs], in0=tt[:], in1=pu[:], op=ALU.mult), [ttb, pub], [hidb])
            for dt in range(8):
                so, sob = ring_next((kout, dt), 0)
                for hf in range(nh):
                    cs = slice(hf * 512, (hf + 1) * 512)
                    po, pob = ps_next()
                    mm_group(po[:], pob, [(so[:, f, :], hid[:, f, cs]) for f in range(FC)], [sob, hidb])
                    op(DVE, lambda po=po, dt=dt, cs=cs: nc.vector.scalar_tensor_tensor(
                        out=x[:, dt, cs], in0=po[:], scalar=0.5, in1=x[:, dt, cs], op0=ALU.mult, op1=ALU.add),
                       [pob, xb], [xb])

        with ExitStack() as a:
            xt, xtb = sb(a, "xt", [128, KC, 1024], F32, True)
            u, ub = sb(a, "uA", [128, KC, 1024], BF16)
            hid, hidb = sb(a, "hidA", [128, FC, 1024], BF16)
            rstd, rstd_b = sb(a, "rstdA", [128, 1024], F32)
            tmpl = [sb(a, f"tmpA{r}", [128, 512], F32) for r in range(4)]
            stg = [sb(a, f"stgA{r}", [128, 4096], BF16, True) for r in range(3)]
            stg_n = [0]

            def stg_next():
                r = stg_n[0] % 3
                stg_n[0] += 1
                return stg[r]

            tmps = {"sq": (u, ub)}
            h1st_ds = new_dsem("d_h1st")
            xsrc = xT.rearrange("(k p) s -> p k s", p=128)
            for i in range(NO):
                t0 = i * 1024
                dma(SP, xtb.dsem, xt[:], xsrc[:, :, t0:t0 + 1024], writes=[xtb])
                rmsnorm(xt, xtb, 0, u, ub, 1024, tmps, rstd, rstd_b)
                ffn(u, ub, xt, xtb, hid, hidb, tmpl, "in1g", "in1u", "out1", 1024)
                dma(POOL, h1st_ds, h1_s[:, :, i * 512:(i + 1) * 512], xt[:, :, 512:1024], reads=[xtb])
                rmsnorm(xt, xtb, 1, u, ub, 1024, tmps, rstd, rstd_b)
                op(DVE, lambda i=i: nc.vector.tensor_copy(out=halo[:, i, :, :], in_=u[:, :, 510:512]), [ub], [halo_b])
                qscale = 1.0 / math.sqrt(128.0)
                for sl in (6, 7):
                    s_, s_b = ring_next(("mix", sl), 0)
                    st, stb = stg_next()
                    stv = st[:, 0:2048].rearrange("p (h t) -> p h t", h=4)
                    for c in range(4):
                        ps, psb = ps_next()
                        mm_group(ps[:], psb, [(s_[:, kc, c * 128:(c + 1) * 128], u[:, kc, 512:1024]) for kc in range(KC)],
                                 [s_b, ub])
                        op(ACT, lambda ps=ps, stv=stv, c=c: nc.scalar.activation(out=stv[:, c, :], in_=ps[:], func=AF.Copy,
                                                                              scale=qscale), [psb], [stb])
                    h0 = (sl - 6) * 4
                    dma(POOL, stb.dsem, qT_s[h0:h0 + 4, :, i * 512:(i + 1) * 512].rearrange("h d s -> d h s"), stv,
                        reads=[stb])
                for sl in (8, 9):
                    s_, s_b = ring_next(("mix", sl), 0)
                    st, stb = stg_next()
                    stv = st[:, 0:4096].rearrange("p (h t) -> p h t", h=4)
                    for c in range(4):
                        for hf in range(2):
                            cs = slice(hf * 512, (hf + 1) * 512)
                            ps, psb = ps_next()
                            mm_group(ps[:], psb, [(s_[:, kc, c * 128:(c + 1) * 128], u[:, kc, cs]) for kc in range(KC)],
                                     [s_b, ub])
                            if hf == 0:
                                op(DVE, lambda ps=ps, stv=stv, c=c, cs=cs: nc.vector.tensor_copy(out=stv[:, c, cs], in_=ps[:]),
                                   [psb], [stb])
                            else:
                                op(ACT, lambda ps=ps, stv=stv, c=c, cs=cs: nc.scalar.copy(out=stv[:, c, cs], in_=ps[:]),
                                   [psb], [stb])
                    h0 = (sl - 8) * 4
                    dma(POOL, stb.dsem, kT_s[h0:h0 + 4, :, t0:t0 + 1024].rearrange("h d s -> d h s"), stv, reads=[stb])
                for sl in (10, 11):
                    s_, s_b = ring_next(("mix", sl), 0)
                    st, stb = stg_next()
                    stv = st[:, 0:4096].rearrange("p (b c) -> p b c", b=8)
                    ch = sl - 10
                    for tb in range(8):
                        ps, psb = ps_next()
                        mm_group(ps[:], psb, [(u[:, kc, tb * 128:(tb + 1) * 128], s_[:, kc, :]) for kc in range(KC)],
                                 [s_b, ub])
                        if tb % 2 == 0:
                            op(DVE, lambda ps=ps, stv=stv, tb=tb: nc.vector.tensor_copy(out=stv[:, tb, :], in_=ps[:]),
                               [psb], [stb])
                        else:
                            op(ACT, lambda ps=ps, stv=stv, tb=tb: nc.scalar.copy(out=stv[:, tb, :], in_=ps[:]),
                               [psb], [stb])
                    dma(POOL, stb.dsem, v_s[:, i * 8:(i + 1) * 8, ch * 512:(ch + 1) * 512], stv, reads=[stb])
                if i == 0:
                    emit_casts(late_casts)
            barrier()

        with ExitStack() as b_:
            kh = [sb(b_, f"kh{r}", [128, S], BF16, True) for r in range(2)]
            vh = [sb(b_, f"vh{r}", [128, NB, 128], BF16, True) for r in range(2)]
            qc, qcb = sb(b_, "qc", [128, H, 512], BF16, True)
            zs = [sb(b_, f"zs{r}", [128, 512], F32) for r in range(4)]
            spb = [sb(b_, f"sp{r}", [128, 512], BF16) for r in range(5)]
            dd = [sb(b_, f"dd{r}", [128, 512], F32) for r in range(2)]
            aa = [sb(b_, f"aa{r}", [128, 512], BF16) for r in range(6)]
            ost = [sb(b_, f"ost{r}", [128, 512], BF16, True) for r in range(2)]
            ob, obb = sb(b_, "obC", [128, H, 512], BF16, True)
            o_events = [[] for _ in range(NO)]
            Zb = [(PS[0], PSb[0]), (PS[1], PSb[1])]
            Xp, Xb = PS[2], PSb[2]
            Op, Opb = PS[3], PSb[3]
            hb, hbb = sb(b_, "hC", [128, KC, 512], F32, True)
            big, bigb = sb(b_, "bigC", [128, FC * 512], BF16, True)
            hid = big[:, :].rearrange("p (f t) -> p f t", f=FC)
            pp = big[:, 0:2 * KC * 514].bitcast(F32).rearrange("p (k t) -> p k t", k=KC)
            f1 = big[:, 0:2 * KC * 512].bitcast(F32).rearrange("p (k t) -> p k t", k=KC)
            u, ub = sb(b_, "uC", [128, KC, 512], BF16)
            mm_, mmb = sb(b_, "mmC", [128, KC, 512], BF16)
            rstd, rstd_b = sb(b_, "rstdC", [128, 512], F32)
            pth, pthb = sb(b_, "pthC", [128, 2, 512], BF16, True)
            tmpl = [sb(b_, f"tmpC{r}", [128, 512], F32) for r in range(3)]
            th, thb = sb(b_, "thC", [128, 4], F32)
            tn = [0]
            cmm = [0]

            def tmp_next():
                r = tn[0] % 3
                tn[0] += 1
                return tmpl[r]

            def cps_next():
                return ps_next(4, 8)

            def g_mm(ps_ap, psb, items, reads, step=4):
                for s_, v_ in _deps(reads, [psb]).items():
                    PE.wait((s_, v_))
                n_ = len(items)
                ins = None
                for t_, (l, r_) in enumerate(items):
                    ins = nc.tensor.matmul(ps_ap, lhsT=l, rhs=r_, start=(t_ == 0), stop=(t_ == n_ - 1),
                                           skip_group_check=True)
                    cmm[0] += 1
                    if t_ < n_ - 1 and (t_ + 1) % step == 0:
                        yield
                ins.then_inc(PE.sem, 1)
                PE.cnt += 1
                _record((PE.sem, PE.cnt), reads, [psb])
                yield

            def g_rmsnorm(x, xb, v, out, outb):
                for kc in range(KC):
                    op(DVE, lambda kc=kc: nc.vector.tensor_tensor(out=u[:, kc, :], in0=x[:, kc, :], in1=x[:, kc, :],
                                                                  op=ALU.mult), [xb], [ub])
                    if kc % 2 == 1:
                        yield
                yield ("pause", 1)
                ps, psb = cps_next()
                yield from g_mm(ps[:], psb, [(ones_h, u[:, kc, :]) for kc in range(KC)], [cst_hb, ub], step=8)
                yield ("pause", 1)
                op(ACT, lambda: nc.scalar.activation(out=rstd[:], in_=ps[:], func=AF.Ln, bias=epsT[:], scale=1.0 / D),
                   [psb, eps_b], [rstd_b])
                op(ACT, lambda: nc.scalar.activation(out=rstd[:], in_=rstd[:], func=AF.Exp, scale=-0.5),
                   [rstd_b], [rstd_b])
                for kc in range(KC):
                    op(DVE, lambda kc=kc: nc.vector.scalar_tensor_tensor(
                        out=out[:, kc, 0:512], in0=x[:, kc, :], scalar=vcol(v, kc), in1=rstd[:],
                        op0=ALU.mult, op1=ALU.mult), [xb, vecs_b, rstd_b], [outb])
                    if kc % 2 == 1:
                        yield
                yield ("pause", 1)

            def g_gate(tt, ttb, pg, pgb):
                op(ACT, lambda: nc.scalar.activation(out=tt[:], in_=pg[:], func=AF.Exp, scale=-1.0), [pgb], [ttb])
                op(DVE, lambda: nc.vector.tensor_scalar(out=tt[:], in0=tt[:], scalar1=1.0, scalar2=None, op0=ALU.add),
                   [ttb], [ttb])
                op(DVE, lambda: nc.vector.reciprocal(out=tt[:], in_=tt[:]), [ttb], [ttb])

            psrc = pT.rearrange("(k p) s -> p k s", p=128)
            osrc = outT.rearrange("(k p) s -> p k s", p=128)

            def c_iter(i):
                cs_i = slice(i * 512, (i + 1) * 512)
                dma(SP, hbb.dsem, hb[:], h1_s[:, :, cs_i], writes=[hbb])
                dma(POOL, pthb.dsem, pth[:], psrc[:, :, cs_i], writes=[pthb])
                yield ("pause", 4)
                yield from g_rmsnorm(hb, hbb, 1, u, ub)
                for hf in range(2):
                    scc, sccb = ring_next(("mix", 2 + hf), 0)
                    scx, scxb = ring_next(("mix", 4 + hf), 1)
                    for c in range(4):
                        ct = hf * 4 + c
                        ws = slice(c * 128, (c + 1) * 128)
                        p1, p1b = cps_next()
                        p2, p2b = cps_next()
                        ph, phb = cps_next()
                        yield from g_mm(p1[:], p1b, [(scc[:, kc, ws], u[:, kc, :]) for kc in range(KC)], [sccb, ub])
                        yield from g_mm(p2[:], p2b, [(scx[:, kc, ws], u[:, kc, :]) for kc in range(KC)], [scxb, ub])
                        fns = []
                        for kc in range(KC):
                            fns.append(lambda kc=kc, ph=ph, scc=scc, ws=ws: nc.tensor.matmul(
                                ph[:, 0:2], lhsT=scc[:, kc, ws], rhs=halo[:, i, kc, :], start=(kc == 0), stop=(kc == KC - 1),
                                skip_group_check=True))
                        for kc in range(KC):
                            fns.append(lambda kc=kc, ph=ph, scx=scx, ws=ws: nc.tensor.matmul(
                                ph[:, 2:4], lhsT=scx[:, kc, ws], rhs=halo[:, i, kc, :], start=False, stop=(kc == KC - 1),
                                skip_group_check=True))
                        op_group(PE, fns, [sccb, scxb, halo_b], [phb])
                        cmm[0] += 2
                        tt, ttb = tmp_next()
                        op(DVE, lambda tt=tt, p1=p1: nc.vector.tensor_copy(out=tt[:], in_=p1[:]), [p1b], [ttb])
                        op(DVE, lambda tt=tt, p2=p2, ct=ct: nc.vector.tensor_tensor(
                            out=pp[:, ct, 2:514], in0=tt[:], in1=p2[:], op=ALU.mult), [ttb, p2b], [bigb])
                        yield
                        op(DVE, lambda ph=ph: nc.vector.tensor_copy(out=th[:], in_=ph[:, 0:4]), [phb], [thb])
                        op(DVE, lambda ct=ct: nc.vector.tensor_tensor(
                            out=pp[:, ct, 0:2], in0=th[:, 0:2], in1=th[:, 2:4], op=ALU.mult), [thb], [bigb])
                        yield
                for hf in range(2):
                    scb, scbb = ring_next(("mix", hf), 0)
                    for c in range(4):
                        ct = hf * 4 + c
                        ws = slice(c * 128, (c + 1) * 128)
                        p1, p1b = cps_next()
                        yield from g_mm(p1[:], p1b, [(scb[:, kc, ws], u[:, kc, :]) for kc in range(KC)], [scbb, ub])
                        tt, ttb = tmp_next()
                        op(DVE, lambda tt=tt, ct=ct: nc.vector.tensor_scalar(
                            out=tt[:], in0=pp[:, ct, 0:512], scalar1=vcol(5, ct), scalar2=None, op0=ALU.mult),
                           [bigb, vecs_b], [ttb])
                        op(DVE, lambda tt=tt, ct=ct: nc.vector.scalar_tensor_tensor(
                            out=tt[:], in0=pp[:, ct, 1:513], scalar=vcol(6, ct), in1=tt[:], op0=ALU.mult, op1=ALU.add),
                           [bigb, vecs_b, ttb], [ttb])
                        yield
                        op(DVE, lambda tt=tt, ct=ct: nc.vector.scalar_tensor_tensor(
                            out=tt[:], in0=pp[:, ct, 2:514], scalar=vcol(7, ct), in1=tt[:], op0=ALU.mult, op1=ALU.add),
                           [bigb, vecs_b, ttb], [ttb])
                        op(DVE, lambda tt=tt, p1=p1, ct=ct: nc.vector.tensor_tensor(
                            out=mm_[:, ct, :], in0=tt[:], in1=p1[:], op=ALU.mult), [ttb, p1b], [mmb])
                        yield
                yield ("pause", 2)
                for hf in range(2):
                    sg, sgb = ring_next(("mix", 12 + hf), 0)
                    sw, swb = ring_next(("co", hf), 1)
                    for c in range(4):
                        ct = hf * 4 + c
                        ws = slice(c * 128, (c + 1) * 128)
                        pg, pgb = cps_next()
                        py, pyb = cps_next()
                        yield from g_mm(pg[:], pgb, [(sg[:, kc, ws], u[:, kc, :]) for kc in range(KC)], [sgb, ub])
                        yield from g_mm(py[:], pyb, [(sw[:, kc, ws], mm_[:, kc, :]) for kc in range(KC)], [swb, mmb])
                        tt, ttb = tmp_next()
                        g_gate(tt, ttb, pg, pgb)
                        yield
                        op(DVE, lambda tt=tt, py=py, ct=ct: nc.vector.tensor_tensor(
                            out=f1[:, ct, :], in0=py[:], in1=tt[:], op=ALU.mult), [ttb, pyb], [bigb])
                        yield
                yield ("need_o", i)
                dma(SP, obb.dsem, ob[:], oT_s[:, :, cs_i], writes=[obb], extra=o_events[i])
                yield ("pause", 2)
                for hf in range(2):
                    sg, sgb = ring_next(("mix", 14 + hf), 0)
                    sw, swb = ring_next(("ao", hf), 1)
                    for c in range(4):
                        ct = hf * 4 + c
                        ws = slice(c * 128, (c + 1) * 128)
                        pg, pgb = cps_next()
                        py, pyb = cps_next()
                        yield from g_mm(pg[:], pgb, [(sg[:, kc, ws], u[:, kc, :]) for kc in range(KC)], [sgb, ub])
                        yield from g_mm(py[:], pyb, [(sw[:, kc, ws], ob[:, kc, :]) for kc in range(KC)], [swb, obb])
                        tt, ttb = tmp_next()
                        g_gate(tt, ttb, pg, pgb)
                        yield
                        op(DVE, lambda tt=tt, py=py: nc.vector.tensor_tensor(
                            out=tt[:], in0=py[:], in1=tt[:], op=ALU.mult), [ttb, pyb], [ttb])
                        op(DVE, lambda tt=tt, ct=ct: nc.vector.tensor_tensor(
                            out=mm_[:, ct, :], in0=tt[:], in1=f1[:, ct, :], op=ALU.add), [ttb, bigb], [mmb])
                        yield
                yield ("pause", 2)
                for hf in range(2):
                    sw, swb = ring_next(("mo", hf), 0)
                    for c in range(4):
                        ct = hf * 4 + c
                        ws = slice(c * 128, (c + 1) * 128)
                        po, pob = cps_next()
                        yield from g_mm(po[:], pob, [(sw[:, kc, ws], mm_[:, kc, :]) for kc in range(KC)], [swb, mmb])
                        op(DVE, lambda po=po, ct=ct: nc.vector.tensor_tensor(
                            out=hb[:, ct, :], in0=po[:], in1=hb[:, ct, :], op=ALU.add), [pob, hbb], [hbb])
                        yield
                yield ("pause", 2)
                yield from g_rmsnorm(hb, hbb, 2, u, ub)
                for j in range(11):
                    sg, sgb = ring_next(("in2g", j), 0)
                    su, sub = ring_next(("in2u", j), 1)
                    for ff in range(2):
                        f = 2 * j + ff
                        fs = slice(ff * 128, (ff + 1) * 128)
                        pg, pgb = cps_next()
                        pu, pub = cps_next()
                        yield from g_mm(pg[:], pgb, [(sg[:, kc, fs], u[:, kc, :]) for kc in range(KC)], [sgb, ub])
                        yield from g_mm(pu[:], pub, [(su[:, kc, fs], u[:, kc, :]) for kc in range(KC)], [sub, ub])
                        tt, ttb = tmp_next()
                        g_gate(tt, ttb, pg, pgb)
                        yield
                        op(DVE, lambda tt=tt, pg=pg: nc.vector.tensor_tensor(
                            out=tt[:], in0=pg[:], in1=tt[:], op=ALU.mult), [ttb, pgb], [ttb])
                        op(DVE, lambda tt=tt, pu=pu, f=f: nc.vector.tensor_tensor(
                            out=hid[:, f, :], in0=tt[:], in1=pu[:], op=ALU.mult), [ttb, pub], [bigb])
                        yield
                yield ("pause", 2)
                for dt in range(8):
                    so, sob = ring_next(("out2", dt), 0)
                    po, pob = cps_next()
                    yield from g_mm(po[:], pob, [(so[:, f, :], hid[:, f, :]) for f in range(FC)], [sob, bigb])
                    op(DVE, lambda po=po, dt=dt: nc.vector.scalar_tensor_tensor(
                        out=hb[:, dt, :], in0=po[:], scalar=0.5, in1=hb[:, dt, :], op0=ALU.mult, op1=ALU.add),
                       [pob, hbb], [hbb])
                    yield
                yield ("pause", 2)
                yield from g_rmsnorm(hb, hbb, 3, u, ub)
                spp, sppb = ring_next(("pp", 0), 0)
                for hf in range(2):
                    sg, sgb = ring_next(("pg", hf), 1 + hf)
                    for c in range(4):
                        ct = hf * 4 + c
                        ws = slice(c * 128, (c + 1) * 128)
                        pg, pgb = cps_next()
                        py, pyb = cps_next()
                        yield from g_mm(pg[:], pgb, [(sg[:, kc, ws], u[:, kc, :]) for kc in range(KC)], [sgb, ub])
                        yield from g_mm(py[:], pyb, [(spp[:, k2, ct * 128:(ct + 1) * 128], pth[:, k2, :]) for k2 in range(2)],
                                        [sppb, pthb])
                        tt, ttb = tmp_next()
                        g_gate(tt, ttb, pg, pgb)
                        yield
                        op(DVE, lambda tt=tt, py=py: nc.vector.tensor_tensor(
                            out=tt[:], in0=py[:], in1=tt[:], op=ALU.mult), [ttb, pyb], [ttb])
                        op(DVE, lambda tt=tt, ct=ct: nc.vector.tensor_tensor(
                            out=hb[:, ct, :], in0=tt[:], in1=hb[:, ct, :], op=ALU.add), [ttb, hbb], [hbb])
                        yield
                yield ("pause", 2)
                yield from g_rmsnorm(hb, hbb, 4, pp, bigb)
                dma(POOL, bigb.dsem, osrc[:, :, cs_i], pp[:, :, 0:512], reads=[bigb])
                yield

            C_MM_EST = 1200.0
            D1, D2 = 2, 6
            NZS, NSP, NDD, NAA = 4, 5, 2, 6

            def chain_ih(c):
                return c // H, c % H

            def load_kv(c):
                i_, h_ = chain_ih(c)
                r_ = c % 2
                L = (2 * i_ + 2) * 512
                dma(POOL, kh[r_][1].dsem, kh[r_][0][:, 0:L], kT_s[h_, :, 0:L], writes=[kh[r_][1]])
                dma(POOL, vh[r_][1].dsem, vh[r_][0][:, 0:L // 128, :], v_s[:, 0:L // 128, h_ * 128:(h_ + 1) * 128],
                    writes=[vh[r_][1]])

            NCHAIN = NO * H
            load_kv(0)
            if NCHAIN > 1:
                load_kv(1)
            G_total = sum(H * (8 * i_ + 8) + D2 for i_ in range(NO))
            C_TOTAL = C_MM_EST * (NO - 1) + 400.0
            cst8 = {"j": 0, "gen": None, "pause": 0, "credit": 0.0, "g": 0}

            def c_advance(avail):
                cst8["g"] += 1
                done = cmm[0]
                cst8["credit"] += max(0.0, C_TOTAL - done) / max(1, G_total - cst8["g"] + 1)
                if cst8["pause"] > 0:
                    cst8["pause"] -= 1
                    return
                burst = 0
                mm0 = cmm[0]
                dve0 = DVE.cnt
                while cst8["credit"] > 0 and burst < 6 and cmm[0] - mm0 < 8 and DVE.cnt - dve0 < 2:
                    if cst8.get("need") is not None:
                        if cst8["need"] >= avail:
                            return
                        cst8["need"] = None
                    if cst8["gen"] is None:
                        if cst8["j"] > avail or cst8["j"] >= NO:
                            return
                        cst8["gen"] = c_iter(cst8["j"])
                    before = cmm[0]
                    try:
                        rv = next(cst8["gen"])
                    except StopIteration:
                        cst8["gen"] = None
                        cst8["j"] += 1
                        continue
                    cst8["credit"] -= (cmm[0] - before)
                    burst += 1
                    if isinstance(rv, tuple):
                        if rv[0] == "need_o":
                            cst8["need"] = rv[1]
                            continue
                        cst8["pause"] = rv[1]
                        return

            for i in range(NO):
                dma(SP, qcb.dsem, qc[:], qT_s[:, :, i * 512:(i + 1) * 512].rearrange("h d s -> d h s"), writes=[qcb])
                top = (2 * i + 1) * 4
                tiles = []
                for gk in range(top + 3, -1, -1):
                    c0 = (gk - top) * 128 if gk >= top else 0
                    tiles.append((gk, c0, gk >= top))
                n = len(tiles)
                flat = [(h, t) for h in range(H) for t in range(n)]
                NF = len(flat)
                def emit_z(si):
                    h, t = flat[si]
                    gk, c0, dg = tiles[t]
                    kt, ktb = kh[(i * H + h) % 2]
                    Zp, Zpb = Zb[si % 2]
                    fz = [lambda: nc.tensor.matmul(
                        Zp[:, c0:512], lhsT=kt[:, gk * 128:(gk + 1) * 128], rhs=qc[:, h, c0:512],
                        start=True, stop=not dg, skip_group_check=True)]
                    if dg:
                        fz.append(lambda: nc.tensor.matmul(
                            Zp[:, c0:c0 + 128], lhsT=ident_h, rhs=negm_h, start=False, stop=True,
                            skip_group_check=True))
                    op_group(PE, fz, [ktb, qcb, cst_hb], [Zpb])

                emit_z(0)
                for s_i in range(NF + D2):
                    if s_i < NF:
                        h, t = flat[s_i]
                        gk, c0, dg = tiles[t]
                        Zp, Zpb = Zb[s_i % 2]
                        z, zb_ = zs[s_i % NZS]
                        sp_, sp_b = spb[s_i % NSP]
                        op(ACT, lambda: nc.scalar.activation(out=z[:, c0:512], in_=Zp[:, c0:512], func=AF.Exp),
                           [Zpb], [zb_])
                        op(ACT, lambda: nc.scalar.activation(out=sp_[:, c0:512], in_=z[:, c0:512], func=AF.Ln,
                                                             bias=oneT[:], scale=1.0), [zb_, one_b], [sp_b])
                    if D1 <= s_i < NF + D1:
                        j = s_i - D1
                        h, t = flat[j]
                        gk, c0, dg = tiles[t]
                        z, zb_ = zs[j % NZS]
                        sp_, sp_b = spb[j % NSP]
                        d_, d_b = dd[j % NDD]
                        a_, a_b = aa[j % NAA]
                        fns = []
                        rds = [cst_hb, sp_b]
                        if t > 0:
                            pgk, pc0, pdg = tiles[t - 1]
                            ps_, ps_b = spb[(j - 1) % NSP]
                            rds.append(ps_b)
                            fns.append(lambda ps_=ps_, pc0=pc0: nc.tensor.matmul(
                                Xp[:, pc0:512], lhsT=M_lt, rhs=ps_[:, pc0:512], start=False, stop=False,
                                skip_group_check=True))
                        fns.append(lambda: nc.tensor.matmul(
                            Xp[:, c0:512], lhsT=M_incl, rhs=sp_[:, c0:512], start=(t == 0), stop=False,
                            skip_group_check=True))
                        op_group(PE, fns, rds, [Xb])
                        op(ACT, lambda: nc.scalar.activation(out=d_[:, c0:512], in_=Xp[:, c0:512], func=AF.Exp,
                                                             scale=-1.0), [Xb], [d_b])
                        op(DVE, lambda: nc.vector.tensor_tensor(
                            out=a_[:, c0:512], in0=z[:, c0:512], in1=d_[:, c0:512], op=ALU.mult), [zb_, d_b], [a_b])
                    if s_i >= D2:
                        j = s_i - D2
                        h, t = flat[j]
                        gk, c0, dg = tiles[t]
                        vt, vtb = vh[(i * H + h) % 2]
                        a_, a_b = aa[j % NAA]
                        op(PE, lambda: nc.tensor.matmul(
                            Op[:, c0:512], lhsT=vt[:, gk, :], rhs=a_[:, c0:512], start=(t == 0), stop=(t == n - 1),
                            skip_group_check=True), [vtb, a_b], [Opb])
                        if t == n - 1:
                            o_, o_b = ost[h % 2]
                            op(DVE, lambda: nc.vector.tensor_copy(out=o_[:], in_=Op[:]), [Opb], [o_b])
                            ev = dma(POOL, o_b.dsem, oT_s[:, h, i * 512:(i + 1) * 512], o_[:], reads=[o_b])
                            o_events[i].append(ev)
                            cnext = i * H + h + 2
                            if cnext < NCHAIN:
                                load_kv(cnext)
                    if s_i + 1 < NF:
                        emit_z(s_i + 1)
                    c_advance(i)
            while cst8["j"] < NO:
                if cst8["gen"] is None:
                    cst8["gen"] = c_iter(cst8["j"])
                for _ in cst8["gen"]:
                    pass
                cst8["gen"] = None
                cst8["j"] += 1
            barrier()

    return nc


def _consts():
    sp = np.arange(128)[:, None]
    s = np.arange(128)[None, :]
    m_incl = (sp >= s).astype(np.float32)
    m_lt = (sp < s).astype(np.float32)
    mask = (sp < s).astype(np.float32)
    negb = (mask - 1.0) * (-NEGBIG)
    ident = np.eye(128, dtype=np.float32)
    return np.ascontiguousarray(np.concatenate([m_incl, m_lt, mask, negb, ident], axis=1), dtype=np.float32)


_CACHE = {}


def kernel(x, p, ffn1_norm, ffn1_w_in, ffn1_w_out, mix_norm, w_mix_in, conv_w, w_conv_out, w_attn_out,
           w_mix_out, ffn2_norm, ffn2_w_in, ffn2_w_out, ple_norm, w_ple_gate, w_ple_proj, final_norm):
    x = np.asarray(x, dtype=np.float32)
    p = np.asarray(p, dtype=np.float32)
    B, S, _ = x.shape
    assert B == 4 and S % 1024 == 0
    NCH = S // 512
    NO = NCH // 2
    if S not in _CACHE:
        _CACHE[S] = build(S)
    nc = _CACHE[S]

    def f(a):
        return np.ascontiguousarray(np.asarray(a, dtype=np.float32))

    vec_list = [ffn1_norm[0], mix_norm[0], ffn2_norm[0], ple_norm[0], final_norm, conv_w[0][0], conv_w[0][1], conv_w[0][2]]
    vecs = np.stack([np.asarray(v, dtype=np.float32).reshape(8, 128).T for v in vec_list], axis=1)
    vecs = np.ascontiguousarray(vecs.reshape(128, 64))
    shared = {
        "w_in1": f(ffn1_w_in[0]), "w_out1": f(ffn1_w_out[0]), "w_mix": f(w_mix_in[0]), "w_co": f(w_conv_out[0]),
        "w_ao": f(w_attn_out[0]), "w_mo": f(w_mix_out[0]), "w_in2": f(ffn2_w_in[0]), "w_out2": f(ffn2_w_out[0]),
        "w_pg": f(w_ple_gate[0]), "w_pp": f(w_ple_proj[0]), "vecs": vecs, "consts": _consts(),
    }
    in_maps = []
    for c in range(8):
        b, pi = c // 2, c % 2
        xs = np.zeros((S, D), dtype=np.float32)
        if pi == 1:
            xs[:] = x[b]
        else:
            xs[512:] = x[b, : S - 512]
        own = np.concatenate([p[0, b, (2 * i + pi) * 512:(2 * i + pi + 1) * 512] for i in range(NO)], axis=0)
        m = dict(shared)
        m["xT"] = np.ascontiguousarray(xs.T)
        m["pT"] = np.ascontiguousarray(own.T)
        in_maps.append(m)
    res = run_bass_kernel_spmd(nc, in_maps, core_ids=list(range(8)))
    out = np.empty((B, S, D), dtype=np.float32)
    for c in range(8):
        b, pi = c // 2, c % 2
        oT = np.asarray(res.results[c]["outT"])
        for i in range(NO):
            out[b, (2 * i + pi) * 512:(2 * i + pi + 1) * 512] = oT[:, i * 512:(i + 1) * 512].T
    return out
```

```python
import math
from contextlib import ExitStack

import numpy as np
import concourse.bass as bass
import concourse.mybir as mybir
from concourse.bass_utils import run_bass_kernel_spmd

F32 = mybir.dt.float32
BF16 = mybir.dt.bfloat16
AF = mybir.ActivationFunctionType
ALU = mybir.AluOpType

D = 1024
H = 8
DFF = 2816
PLE = 256
KC = 8
FC = 22
EPS = 1e-6
NEGBIG = -30000.0
RING = 3
SLOT_ELEMS = 4096


class Buf:
    __slots__ = ("name", "w", "r", "dsem")

    def __init__(self, name):
        self.name = name
        self.w = None
        self.r = {}
        self.dsem = None


class DSem:
    __slots__ = ("sem", "cnt")

    def __init__(self, sem):
        self.sem = sem
        self.cnt = 0


class Eng:
    def __init__(self, handle, sem, selfwait):
        self.h = handle
        self.sem = sem
        self.cnt = 0
        self.waited = {}
        self.selfwait = selfwait

    def wait(self, ev):
        sem, val = ev
        if sem is self.sem and not self.selfwait:
            return
        if self.waited.get(sem, 0) >= val:
            return
        self.h.wait_ge(sem, val)
        self.waited[sem] = val


def _deps(reads, writes):
    deps = {}
    for b in reads:
        if b.w is not None:
            s, v = b.w
            if deps.get(s, 0) < v:
                deps[s] = v
    for b in writes:
        if b.w is not None:
            s, v = b.w
            if deps.get(s, 0) < v:
                deps[s] = v
        for s, v in b.r.items():
            if deps.get(s, 0) < v:
                deps[s] = v
    return deps


def _record(ev, reads, writes):
    s, v = ev
    for b in reads:
        if b.r.get(s, 0) < v:
            b.r[s] = v
    for b in writes:
        b.w = ev
        b.r = {}


def op(eng, fn, reads=(), writes=()):
    for s, v in _deps(reads, writes).items():
        eng.wait((s, v))
    ins = fn()
    ins.then_inc(eng.sem, 1)
    eng.cnt += 1
    ev = (eng.sem, eng.cnt)
    _record(ev, reads, writes)
    return ev


def op_group(eng, fns, reads=(), writes=()):
    for s, v in _deps(reads, writes).items():
        eng.wait((s, v))
    ins = None
    for fn in fns:
        ins = fn()
    ins.then_inc(eng.sem, 1)
    eng.cnt += 1
    ev = (eng.sem, eng.cnt)
    _record(ev, reads, writes)
    return ev


def dma(q, ds, out_ap, in_ap, reads=(), writes=(), extra=()):
    deps = _deps(reads, writes)
    for s_, v_ in extra:
        if deps.get(s_, 0) < v_:
            deps[s_] = v_
    if ds.cnt:
        if deps.get(ds.sem, 0) < ds.cnt:
            deps[ds.sem] = ds.cnt
    for s, v in deps.items():
        q.wait((s, v))
    q.h.dma_start(out=out_ap, in_=in_ap).then_inc(ds.sem, 16)
    ds.cnt += 16
    ev = (ds.sem, ds.cnt)
    _record(ev, reads, writes)
    return ev


def build(S):
    NCH = S // 512
    NO = NCH // 2
    NB = S // 128
    SO = S // 2
    nc = bass.Bass("TRN2", target_bir_lowering=False)

    def din(name, shape, dt=F32):
        return nc.dram_tensor(name, list(shape), dt, kind="ExternalInput").ap()

    xT = din("xT", [D, S])
    pT = din("pT", [PLE, SO])
    W = {
        "in1": din("w_in1", [D, 2 * DFF]),
        "out1": din("w_out1", [DFF, D]),
        "mix": din("w_mix", [D, 8 * D]),
        "co": din("w_co", [D, D]),
        "ao": din("w_ao", [D, D]),
        "mo": din("w_mo", [D, D]),
        "in2": din("w_in2", [D, 2 * DFF]),
        "out2": din("w_out2", [DFF, D]),
        "pg": din("w_pg", [D, D]),
        "pp": din("w_pp", [PLE, D]),
    }
    vecs_d = din("vecs", [128, 64])
    consts_d = din("consts", [128, 640])
    outT = nc.dram_tensor("outT", [D, SO], F32, kind="ExternalOutput").ap()

    kT_s = nc.dram_tensor("kT_s", [H, 128, S], BF16).ap()
    qT_s = nc.dram_tensor("qT_s", [H, 128, SO], BF16).ap()
    v_s = nc.dram_tensor("v_s", [128, NB, D], BF16).ap()
    h1_s = nc.dram_tensor("h1_s", [128, KC, SO], F32).ap()
    oT_s = nc.dram_tensor("oT_s", [128, H, SO], BF16).ap()

    slabs = {}

    def add_slabs(prefix, w_ap, col0, ncols, cw, nm):
        K = w_ap.shape[0]
        kc = K // 128
        ns = ncols // cw
        wb = nc.dram_tensor("wb_" + nm, [ns, 128, kc, cw], BF16).ap()
        src = w_ap.rearrange("(k p) n -> p k n", p=128)
        for j in range(ns):
            slabs[(prefix, j)] = (src[:, :, col0 + j * cw: col0 + (j + 1) * cw], wb[j], kc, cw)

    add_slabs("in1g", W["in1"], 0, DFF, 256, "in1g")
    add_slabs("in1u", W["in1"], DFF, DFF, 256, "in1u")
    add_slabs("out1", W["out1"], 0, D, 128, "out1")
    add_slabs("mix", W["mix"], 0, 8 * D, 512, "mix")
    add_slabs("co", W["co"], 0, D, 512, "co")
    add_slabs("ao", W["ao"], 0, D, 512, "ao")
    add_slabs("mo", W["mo"], 0, D, 512, "mo")
    add_slabs("in2g", W["in2"], 0, DFF, 256, "in2g")
    add_slabs("in2u", W["in2"], DFF, DFF, 256, "in2u")
    add_slabs("out2", W["out2"], 0, D, 128, "out2")
    add_slabs("pg", W["pg"], 0, D, 512, "pg")
    add_slabs("pp", W["pp"], 0, D, 1024, "pp")

    plan = []
    for i in range(NO):
        for j in range(11):
            plan += [("in1g", j), ("in1u", j)]
        plan += [("out1", dt) for dt in range(8)]
        plan += [("mix", s) for s in (6, 7, 8, 9, 10, 11)]
    for i in range(NO):
        plan += [("mix", 2), ("mix", 4), ("mix", 3), ("mix", 5)]
        plan += [("mix", 0), ("mix", 1)]
        plan += [("mix", 12), ("co", 0), ("mix", 13), ("co", 1)]
        plan += [("mix", 14), ("ao", 0), ("mix", 15), ("ao", 1)]
        plan += [("mo", 0), ("mo", 1)]
        for j in range(11):
            plan += [("in2g", j), ("in2u", j)]
        plan += [("out2", dt) for dt in range(8)]
        plan += [("pp", 0), ("pg", 0), ("pg", 1)]

    with ExitStack() as g:
        def sem(name):
            return g.enter_context(nc.semaphore(name))

        PE = Eng(nc.tensor, sem("s_pe"), False)
        ACT = Eng(nc.scalar, sem("s_act"), True)
        DVE = Eng(nc.vector, sem("s_dve"), True)
        POOL = Eng(nc.gpsimd, sem("s_pool"), True)
        SP = Eng(nc.sync, sem("s_sp"), False)
        engines = [PE, ACT, DVE, POOL, SP]
        dsems = []

        def new_dsem(name):
            ds = DSem(sem(name))
            dsems.append(ds)
            return ds

        def barrier():
            for e in engines:
                for f in engines:
                    if f is not e and f.cnt:
                        e.wait((f.sem, f.cnt))
                for ds in dsems:
                    if ds.cnt:
                        e.wait((ds.sem, ds.cnt))

        def sb(es, name, shape, dt, with_dsem=False):
            t = es.enter_context(nc.sbuf_tensor("sb_" + name, list(shape), dt))
            b = Buf(name)
            if with_dsem:
                b.dsem = new_dsem("d_" + name)
            return t, b

        vecs, vecs_b = sb(g, "vecs", [128, 64], F32, True)
        cst_f, cst_fb = sb(g, "cst_f", [128, 640], F32, True)
        cst_h, cst_hb = sb(g, "cst_h", [128, 640], BF16)
        epsT, eps_b = sb(g, "epsT", [128, 1], F32)
        halo, halo_b = sb(g, "halo", [128, NO, KC, 2], BF16)
        ring_t, ring_b = [], []
        for r in range(RING):
            t, b = sb(g, f"ring{r}", [128, SLOT_ELEMS], BF16, True)
            ring_t.append(t)
            ring_b.append(b)
        PS, PSb = [], []
        for r in range(8):
            t = g.enter_context(nc.psum_tensor(f"ps{r}", [128, 512], F32))
            PS.append(t)
            PSb.append(Buf(f"ps{r}"))

        dma(SP, vecs_b.dsem, vecs[:], vecs_d, writes=[vecs_b])
        dma(SP, cst_fb.dsem, cst_f[:], consts_d, writes=[cst_fb])
        op(DVE, lambda: nc.vector.tensor_copy(out=cst_h[:, 0:256], in_=cst_f[:, 0:256]), [cst_fb], [cst_hb])
        op(DVE, lambda: nc.vector.memset(cst_h[:, 256:384], 1.0), [], [cst_hb])
        op(DVE, lambda: nc.vector.tensor_copy(out=cst_h[:, 384:640], in_=cst_f[:, 384:640]), [cst_fb], [cst_hb])
        op(DVE, lambda: nc.vector.memset(epsT[:], EPS), [], [eps_b])
        oneT, one_b = sb(g, "oneT", [128, 1], F32)
        op(DVE, lambda: nc.vector.memset(oneT[:], 1.0), [], [one_b])
        M_incl = cst_h[:, 0:128]
        M_lt = cst_h[:, 128:256]
        ones_h = cst_h[:, 256:384]
        negm_h = cst_h[:, 384:512]
        ident_h = cst_h[:, 512:640]
        mask_f = cst_f[:, 256:384]
        negb_f = cst_f[:, 384:512]

        def vcol(v, kc):
            return vecs[:, v * 8 + kc: v * 8 + kc + 1]

        cast_ds = [new_dsem(f"d_cast{r}") for r in range(4)]
        slab_buf = {}
        order = []
        for k in plan:
            if k not in slab_buf:
                slab_buf[k] = Buf("slab")
                order.append(k)
        for n, k in enumerate(order):
            src, dst, kc, cw = slabs[k]
            dma(POOL, cast_ds[n % 4], dst, src, writes=[slab_buf[k]])

        ring_state = {"idx": 0, "issued": 0}

        def ring_next(key, hold=0):
            idx = ring_state["idx"]
            assert plan[idx] == key, (idx, plan[idx], key)
            lim = min(len(plan), idx - hold + RING)
            while ring_state["issued"] < lim:
                j = ring_state["issued"]
                src, dst, kc, cw = slabs[plan[j]]
                sl = j % RING
                view = ring_t[sl][:, 0:kc * cw].rearrange("p (k c) -> p k c", k=kc)
                dma(SP, ring_b[sl].dsem, view, dst, reads=[slab_buf[plan[j]]], writes=[ring_b[sl]])
                ring_state["issued"] += 1
            assert ring_state["issued"] > idx
            ring_state["idx"] += 1
            src, dst, kc, cw = slabs[key]
            sl = idx % RING
            return ring_t[sl][:, 0:kc * cw].rearrange("p (k c) -> p k c", k=kc), ring_b[sl]

        ps_state = {"n": 0}

        def ps_next(lo=0, hi=8):
            n = ps_state["n"]
            ps_state["n"] += 1
            r = lo + n % (hi - lo)
            return PS[r], PSb[r]

        def mm_group(ps, psb, items, reads):
            n = len(items)
            fns = []
            for t, (l, r_) in enumerate(items):
                fns.append(lambda l=l, r_=r_, t=t: nc.tensor.matmul(
                    ps, lhsT=l, rhs=r_, start=(t == 0), stop=(t == n - 1), skip_group_check=True))
            return op_group(PE, fns, reads, [psb])

        def rmsnorm(x, xb, v, u, ub, Wd, tmps, rstd, rstd_b):
            sq, sqb = tmps["sq"]
            for kc in range(KC):
                op(ACT, lambda kc=kc: nc.scalar.activation(out=sq[:, kc, 0:Wd], in_=x[:, kc, 0:Wd], func=AF.Square),
                   [xb], [sqb])
            for hf in range(Wd // 512):
                ps, psb = ps_next()
                cs = slice(hf * 512, (hf + 1) * 512)
                mm_group(ps[:], psb, [(ones_h, sq[:, kc, cs]) for kc in range(KC)], [cst_hb, sqb])
                op(ACT, lambda ps=ps, cs=cs: nc.scalar.activation(out=rstd[:, cs], in_=ps[:], func=AF.Sqrt,
                                                                  bias=epsT[:], scale=1.0 / D),
                   [psb, eps_b], [rstd_b])
                op(DVE, lambda cs=cs: nc.vector.reciprocal(out=rstd[:, cs], in_=rstd[:, cs]), [rstd_b], [rstd_b])
            for kc in range(KC):
                op(DVE, lambda kc=kc: nc.vector.scalar_tensor_tensor(
                    out=u[:, kc, 0:Wd], in0=x[:, kc, 0:Wd], scalar=vcol(v, kc), in1=rstd[:, 0:Wd],
                    op0=ALU.mult, op1=ALU.mult), [xb, vecs_b, rstd_b], [ub])

        def ffn(u, ub, x, xb, hid, hidb, tmpl, kin_g, kin_u, kout, Wd):
            nh = Wd // 512
            tcount = 0
            for j in range(11):
                sg, sgb = ring_next((kin_g, j), 0)
                su, sub = ring_next((kin_u, j), 1)
                for ff in range(2):
                    f = 2 * j + ff
                    fs = slice(ff * 128, (ff + 1) * 128)
                    for hf in range(nh):
                        cs = slice(hf * 512, (hf + 1) * 512)
                        pg, pgb = ps_next()
                        pu, pub = ps_next()
                        mm_group(pg[:], pgb, [(sg[:, kc, fs], u[:, kc, cs]) for kc in range(KC)], [sgb, ub])
                        mm_group(pu[:], pub, [(su[:, kc, fs], u[:, kc, cs]) for kc in range(KC)], [sub, ub])
                        tt, ttb = tmpl[tcount % len(tmpl)]
                        tcount += 1
                        op(ACT, lambda tt=tt, pg=pg: nc.scalar.activation(out=tt[:], in_=pg[:], func=AF.Silu),
                           [pgb], [ttb])
                        op(DVE, lambda tt=tt, pu=pu, f=f, cs=cs: nc.vector.tensor_tensor(
                            out=hid[:, f, cs], in0=tt[:], in1=pu[:], op=ALU.mult), [ttb, pub], [hidb])
            for dt in range(8):
                so, sob = ring_next((kout, dt), 0)
                for hf in range(nh):
                    cs = slice(hf * 512, (hf + 1) * 512)
                    po, pob = ps_next()
                    mm_group(po[:], pob, [(so[:, f, :], hid[:, f, cs]) for f in range(FC)], [sob, hidb])
                    op(DVE, lambda po=po, dt=dt, cs=cs: nc.vector.scalar_tensor_tensor(
                        out=x[:, dt, cs], in0=po[:], scalar=0.5, in1=x[:, dt, cs], op0=ALU.mult, op1=ALU.add),
                       [pob, xb], [xb])

        with ExitStack() as a:
            xt, xtb = sb(a, "xt", [128, KC, 1024], F32, True)
            u, ub = sb(a, "uA", [128, KC, 1024], BF16)
            hid, hidb = sb(a, "hidA", [128, FC, 1024], BF16)
            rstd, rstd_b = sb(a, "rstdA", [128, 1024], F32)
            tmpl = [sb(a, f"tmpA{r}", [128, 512], F32) for r in range(4)]
            stg = [sb(a, f"stgA{r}", [128, 4096], BF16, True) for r in range(3)]
            stg_n = [0]

            def stg_next():
                r = stg_n[0] % 3
                stg_n[0] += 1
                return stg[r]

            tmps = {"sq": (u, ub)}
            h1st_ds = new_dsem("d_h1st")
            xsrc = xT.rearrange("(k p) s -> p k s", p=128)
            for i in range(NO):
                t0 = i * 1024
                dma(SP, xtb.dsem, xt[:], xsrc[:, :, t0:t0 + 1024], writes=[xtb])
                rmsnorm(xt, xtb, 0, u, ub, 1024, tmps, rstd, rstd_b)
                ffn(u, ub, xt, xtb, hid, hidb, tmpl, "in1g", "in1u", "out1", 1024)
                dma(POOL, h1st_ds, h1_s[:, :, i * 512:(i + 1) * 512], xt[:, :, 512:1024], reads=[xtb])
                rmsnorm(xt, xtb, 1, u, ub, 1024, tmps, rstd, rstd_b)
                op(DVE, lambda i=i: nc.vector.tensor_copy(out=halo[:, i, :, :], in_=u[:, :, 510:512]), [ub], [halo_b])
                qscale = 1.0 / math.sqrt(128.0)
                for sl in (6, 7):
                    s_, s_b = ring_next(("mix", sl), 0)
                    st, stb = stg_next()
                    stv = st[:, 0:2048].rearrange("p (h t) -> p h t", h=4)
                    for c in range(4):
                        ps, psb = ps_next()
                        mm_group(ps[:], psb, [(s_[:, kc, c * 128:(c + 1) * 128], u[:, kc, 512:1024]) for kc in range(KC)],
                                 [s_b, ub])
                        op(ACT, lambda ps=ps, stv=stv, c=c: nc.scalar.activation(out=stv[:, c, :], in_=ps[:], func=AF.Copy,
                                                                              scale=qscale), [psb], [stb])
                    h0 = (sl - 6) * 4
                    dma(POOL, stb.dsem, qT_s[h0:h0 + 4, :, i * 512:(i + 1) * 512].rearrange("h d s -> d h s"), stv,
                        reads=[stb])
                for sl in (8, 9):
                    s_, s_b = ring_next(("mix", sl), 0)
                    st, stb = stg_next()
                    stv = st[:, 0:4096].rearrange("p (h t) -> p h t", h=4)
                    for c in range(4):
                        for hf in range(2):
                            cs = slice(hf * 512, (hf + 1) * 512)
                            ps, psb = ps_next()
                            mm_group(ps[:], psb, [(s_[:, kc, c * 128:(c + 1) * 128], u[:, kc, cs]) for kc in range(KC)],
                                     [s_b, ub])
                            if hf == 0:
                                op(DVE, lambda ps=ps, stv=stv, c=c, cs=cs: nc.vector.tensor_copy(out=stv[:, c, cs], in_=ps[:]),
                                   [psb], [stb])
                            else:
                                op(ACT, lambda ps=ps, stv=stv, c=c, cs=cs: nc.scalar.copy(out=stv[:, c, cs], in_=ps[:]),
                                   [psb], [stb])
                    h0 = (sl - 8) * 4
                    dma(POOL, stb.dsem, kT_s[h0:h0 + 4, :, t0:t0 + 1024].rearrange("h d s -> d h s"), stv, reads=[stb])
                for sl in (10, 11):
                    s_, s_b = ring_next(("mix", sl), 0)
                    st, stb = stg_next()
                    stv = st[:, 0:4096].rearrange("p (b c) -> p b c", b=8)
                    ch = sl - 10
                    for tb in range(8):
                        ps, psb = ps_next()
                        mm_group(ps[:], psb, [(u[:, kc, tb * 128:(tb + 1) * 128], s_[:, kc, :]) for kc in range(KC)],
                                 [s_b, ub])
                        if tb % 2 == 0:
                            op(DVE, lambda ps=ps, stv=stv, tb=tb: nc.vector.tensor_copy(out=stv[:, tb, :], in_=ps[:]),
                               [psb], [stb])
                        else:
                            op(ACT, lambda ps=ps, stv=stv, tb=tb: nc.scalar.copy(out=stv[:, tb, :], in_=ps[:]),
                               [psb], [stb])
                    dma(POOL, stb.dsem, v_s[:, i * 8:(i + 1) * 8, ch * 512:(ch + 1) * 512], stv, reads=[stb])
            barrier()

        with ExitStack() as b_:
            kh = [sb(b_, f"kh{r}", [128, S], BF16, True) for r in range(2)]
            vh = [sb(b_, f"vh{r}", [128, NB, 128], BF16, True) for r in range(2)]
            qc, qcb = sb(b_, "qc", [128, H, 512], BF16, True)
            zs = [sb(b_, f"zs{r}", [128, 512], F32) for r in range(4)]
            spb = [sb(b_, f"sp{r}", [128, 512], BF16) for r in range(5)]
            dd = [sb(b_, f"dd{r}", [128, 512], F32) for r in range(2)]
            aa = [sb(b_, f"aa{r}", [128, 512], BF16) for r in range(6)]
            ost = [sb(b_, f"ost{r}", [128, 512], BF16, True) for r in range(2)]
            ob, obb = sb(b_, "obC", [128, H, 512], BF16, True)
            o_events = [[] for _ in range(NO)]
            Zb = [(PS[0], PSb[0]), (PS[1], PSb[1])]
            Xp, Xb = PS[2], PSb[2]
            Op, Opb = PS[3], PSb[3]
            hb, hbb = sb(b_, "hC", [128, KC, 512], F32, True)
            big, bigb = sb(b_, "bigC", [128, FC * 512], BF16, True)
            hid = big[:, :].rearrange("p (f t) -> p f t", f=FC)
            pp = big[:, 0:2 * KC * 514].bitcast(F32).rearrange("p (k t) -> p k t", k=KC)
            f1 = big[:, 0:2 * KC * 512].bitcast(F32).rearrange("p (k t) -> p k t", k=KC)
            u, ub = sb(b_, "uC", [128, KC, 512], BF16)
            mm_, mmb = sb(b_, "mmC", [128, KC, 512], BF16)
            rstd, rstd_b = sb(b_, "rstdC", [128, 512], F32)
            pth, pthb = sb(b_, "pthC", [128, 2, 512], BF16, True)
            tmpl = [sb(b_, f"tmpC{r}", [128, 512], F32) for r in range(3)]
            th, thb = sb(b_, "thC", [128, 4], F32)
            tn = [0]
            cmm = [0]

            def tmp_next():
                r = tn[0] % 3
                tn[0] += 1
                return tmpl[r]

            def cps_next():
                return ps_next(4, 8)

            def g_mm(ps_ap, psb, items, reads, step=4):
                for s_, v_ in _deps(reads, [psb]).items():
                    PE.wait((s_, v_))
                n_ = len(items)
                ins = None
                for t_, (l, r_) in enumerate(items):
                    ins = nc.tensor.matmul(ps_ap, lhsT=l, rhs=r_, start=(t_ == 0), stop=(t_ == n_ - 1),
                                           skip_group_check=True)
                    cmm[0] += 1
                    if t_ < n_ - 1 and (t_ + 1) % step == 0:
                        yield
                ins.then_inc(PE.sem, 1)
                PE.cnt += 1
                _record((PE.sem, PE.cnt), reads, [psb])
                yield

            def g_rmsnorm(x, xb, v, out, outb):
                for kc in range(KC):
                    op(DVE, lambda kc=kc: nc.vector.tensor_tensor(out=u[:, kc, :], in0=x[:, kc, :], in1=x[:, kc, :],
                                                                  op=ALU.mult), [xb], [ub])
                    if kc % 2 == 1:
                        yield
                yield ("pause", 1)
                ps, psb = cps_next()
                yield from g_mm(ps[:], psb, [(ones_h, u[:, kc, :]) for kc in range(KC)], [cst_hb, ub], step=8)
                yield ("pause", 1)
                op(ACT, lambda: nc.scalar.activation(out=rstd[:], in_=ps[:], func=AF.Ln, bias=epsT[:], scale=1.0 / D),
                   [psb, eps_b], [rstd_b])
                op(ACT, lambda: nc.scalar.activation(out=rstd[:], in_=rstd[:], func=AF.Exp, scale=-0.5),
                   [rstd_b], [rstd_b])
                for kc in range(KC):
                    op(DVE, lambda kc=kc: nc.vector.scalar_tensor_tensor(
                        out=out[:, kc, 0:512], in0=x[:, kc, :], scalar=vcol(v, kc), in1=rstd[:],
                        op0=ALU.mult, op1=ALU.mult), [xb, vecs_b, rstd_b], [outb])
                    if kc % 2 == 1:
                        yield
                yield ("pause", 1)

            def g_gate(tt, ttb, pg, pgb):
                op(ACT, lambda: nc.scalar.activation(out=tt[:], in_=pg[:], func=AF.Exp, scale=-1.0), [pgb], [ttb])
                op(DVE, lambda: nc.vector.tensor_scalar(out=tt[:], in0=tt[:], scalar1=1.0, scalar2=None, op0=ALU.add),
                   [ttb], [ttb])
                op(DVE, lambda: nc.vector.reciprocal(out=tt[:], in_=tt[:]), [ttb], [ttb])

            psrc = pT.rearrange("(k p) s -> p k s", p=128)
            osrc = outT.rearrange("(k p) s -> p k s", p=128)

            def c_iter(i):
                cs_i = slice(i * 512, (i + 1) * 512)
                dma(SP, hbb.dsem, hb[:], h1_s[:, :, cs_i], writes=[hbb])
                dma(POOL, pthb.dsem, pth[:], psrc[:, :, cs_i], writes=[pthb])
                yield ("pause", 4)
                yield from g_rmsnorm(hb, hbb, 1, u, ub)
                for hf in range(2):
                    scc, sccb = ring_next(("mix", 2 + hf), 0)
                    scx, scxb = ring_next(("mix", 4 + hf), 1)
                    for c in range(4):
                        ct = hf * 4 + c
                        ws = slice(c * 128, (c + 1) * 128)
                        p1, p1b = cps_next()
                        p2, p2b = cps_next()
                        ph, phb = cps_next()
                        yield from g_mm(p1[:], p1b, [(scc[:, kc, ws], u[:, kc, :]) for kc in range(KC)], [sccb, ub])
                        yield from g_mm(p2[:], p2b, [(scx[:, kc, ws], u[:, kc, :]) for kc in range(KC)], [scxb, ub])
                        fns = []
                        for kc in range(KC):
                            fns.append(lambda kc=kc, ph=ph, scc=scc, ws=ws: nc.tensor.matmul(
                                ph[:, 0:2], lhsT=scc[:, kc, ws], rhs=halo[:, i, kc, :], start=(kc == 0), stop=(kc == KC - 1),
                                skip_group_check=True))
                        for kc in range(KC):
                            fns.append(lambda kc=kc, ph=ph, scx=scx, ws=ws: nc.tensor.matmul(
                                ph[:, 2:4], lhsT=scx[:, kc, ws], rhs=halo[:, i, kc, :], start=False, stop=(kc == KC - 1),
                                skip_group_check=True))
                        op_group(PE, fns, [sccb, scxb, halo_b], [phb])
                        cmm[0] += 2
                        tt, ttb = tmp_next()
                        op(DVE, lambda tt=tt, p1=p1: nc.vector.tensor_copy(out=tt[:], in_=p1[:]), [p1b], [ttb])
                        op(DVE, lambda tt=tt, p2=p2, ct=ct: nc.vector.tensor_tensor(
                            out=pp[:, ct, 2:514], in0=tt[:], in1=p2[:], op=ALU.mult), [ttb, p2b], [bigb])
                        yield
                        op(DVE, lambda ph=ph: nc.vector.tensor_copy(out=th[:], in_=ph[:, 0:4]), [phb], [thb])
                        op(DVE, lambda ct=ct: nc.vector.tensor_tensor(
                            out=pp[:, ct, 0:2], in0=th[:, 0:2], in1=th[:, 2:4], op=ALU.mult), [thb], [bigb])
                        yield
                for hf in range(2):
                    scb, scbb = ring_next(("mix", hf), 0)
                    for c in range(4):
                        ct = hf * 4 + c
                        ws = slice(c * 128, (c + 1) * 128)
                        p1, p1b = cps_next()
                        yield from g_mm(p1[:], p1b, [(scb[:, kc, ws], u[:, kc, :]) for kc in range(KC)], [scbb, ub])
                        tt, ttb = tmp_next()
                        op(DVE, lambda tt=tt, ct=ct: nc.vector.tensor_scalar(
                            out=tt[:], in0=pp[:, ct, 0:512], scalar1=vcol(5, ct), scalar2=None, op0=ALU.mult),
                           [bigb, vecs_b], [ttb])
                        op(DVE, lambda tt=tt, ct=ct: nc.vector.scalar_tensor_tensor(
                            out=tt[:], in0=pp[:, ct, 1:513], scalar=vcol(6, ct), in1=tt[:], op0=ALU.mult, op1=ALU.add),
                           [bigb, vecs_b, ttb], [ttb])
                        yield
                        op(DVE, lambda tt=tt, ct=ct: nc.vector.scalar_tensor_tensor(
                            out=tt[:], in0=pp[:, ct, 2:514], scalar=vcol(7, ct), in1=tt[:], op0=ALU.mult, op1=ALU.add),
                           [bigb, vecs_b, ttb], [ttb])
                        op(DVE, lambda tt=tt, p1=p1, ct=ct: nc.vector.tensor_tensor(
                            out=mm_[:, ct, :], in0=tt[:], in1=p1[:], op=ALU.mult), [ttb, p1b], [mmb])
                        yield
                yield ("pause", 2)
                for hf in range(2):
                    sg, sgb = ring_next(("mix", 12 + hf), 0)
                    sw, swb = ring_next(("co", hf), 1)
                    for c in range(4):
                        ct = hf * 4 + c
                        ws = slice(c * 128, (c + 1) * 128)
                        pg, pgb = cps_next()
                        py, pyb = cps_next()
                        yield from g_mm(pg[:], pgb, [(sg[:, kc, ws], u[:, kc, :]) for kc in range(KC)], [sgb, ub])
                        yield from g_mm(py[:], pyb, [(sw[:, kc, ws], mm_[:, kc, :]) for kc in range(KC)], [swb, mmb])
                        tt, ttb = tmp_next()
                        g_gate(tt, ttb, pg, pgb)
                        yield
                        op(DVE, lambda tt=tt, py=py, ct=ct: nc.vector.tensor_tensor(
                            out=f1[:, ct, :], in0=py[:], in1=tt[:], op=ALU.mult), [ttb, pyb], [bigb])
                        yield
                yield ("need_o", i)
                dma(SP, obb.dsem, ob[:], oT_s[:, :, cs_i], writes=[obb], extra=o_events[i])
                yield ("pause", 2)
                for hf in range(2):
                    sg, sgb = ring_next(("mix", 14 + hf), 0)
                    sw, swb = ring_next(("ao", hf), 1)
                    for c in range(4):
                        ct = hf * 4 + c
                        ws = slice(c * 128, (c + 1) * 128)
                        pg, pgb = cps_next()
                        py, pyb = cps_next()
                        yield from g_mm(pg[:], pgb, [(sg[:, kc, ws], u[:, kc, :]) for kc in range(KC)], [sgb, ub])
                        yield from g_mm(py[:], pyb, [(sw[:, kc, ws], ob[:, kc, :]) for kc in range(KC)], [swb, obb])
                        tt, ttb = tmp_next()
                        g_gate(tt, ttb, pg, pgb)
                        yield
                        op(DVE, lambda tt=tt, py=py: nc.vector.tensor_tensor(
                            out=tt[:], in0=py[:], in1=tt[:], op=ALU.mult), [ttb, pyb], [ttb])
                        op(DVE, lambda tt=tt, ct=ct: nc.vector.tensor_tensor(
                            out=mm_[:, ct, :], in0=tt[:], in1=f1[:, ct, :], op=ALU.add), [ttb, bigb], [mmb])
                        yield
                yield ("pause", 2)
                for hf in range(2):
                    sw, swb = ring_next(("mo", hf), 0)
                    for c in range(4):
                        ct = hf * 4 + c
                        ws = slice(c * 128, (c + 1) * 128)
                        po, pob = cps_next()
                        yield from g_mm(po[:], pob, [(sw[:, kc, ws], mm_[:, kc, :]) for kc in range(KC)], [swb, mmb])
                        op(DVE, lambda po=po, ct=ct: nc.vector.tensor_tensor(
                            out=hb[:, ct, :], in0=po[:], in1=hb[:, ct, :], op=ALU.add), [pob, hbb], [hbb])
                        yield
                yield ("pause", 2)
                yield from g_rmsnorm(hb, hbb, 2, u, ub)
                for j in range(11):
                    sg, sgb = ring_next(("in2g", j), 0)
                    su, sub = ring_next(("in2u", j), 1)
                    for ff in range(2):
                        f = 2 * j + ff
                        fs = slice(ff * 128, (ff + 1) * 128)
                        pg, pgb = cps_next()
                        pu, pub = cps_next()
                        yield from g_mm(pg[:], pgb, [(sg[:, kc, fs], u[:, kc, :]) for kc in range(KC)], [sgb, ub])
                        yield from g_mm(pu[:], pub, [(su[:, kc, fs], u[:, kc, :]) for kc in range(KC)], [sub, ub])
                        tt, ttb = tmp_next()
                        g_gate(tt, ttb, pg, pgb)
                        yield
                        op(DVE, lambda tt=tt, pg=pg: nc.vector.tensor_tensor(
                            out=tt[:], in0=pg[:], in1=tt[:], op=ALU.mult), [ttb, pgb], [ttb])
                        op(DVE, lambda tt=tt, pu=pu, f=f: nc.vector.tensor_tensor(
                            out=hid[:, f, :], in0=tt[:], in1=pu[:], op=ALU.mult), [ttb, pub], [bigb])
                        yield
                yield ("pause", 2)
                for dt in range(8):
                    so, sob = ring_next(("out2", dt), 0)
                    po, pob = cps_next()
                    yield from g_mm(po[:], pob, [(so[:, f, :], hid[:, f, :]) for f in range(FC)], [sob, bigb])
                    op(DVE, lambda po=po, dt=dt: nc.vector.scalar_tensor_tensor(
                        out=hb[:, dt, :], in0=po[:], scalar=0.5, in1=hb[:, dt, :], op0=ALU.mult, op1=ALU.add),
                       [pob, hbb], [hbb])
                    yield
                yield ("pause", 2)
                yield from g_rmsnorm(hb, hbb, 3, u, ub)
                spp, sppb = ring_next(("pp", 0), 0)
                for hf in range(2):
                    sg, sgb = ring_next(("pg", hf), 1 + hf)
                    for c in range(4):
                        ct = hf * 4 + c
                        ws = slice(c * 128, (c + 1) * 128)
                        pg, pgb = cps_next()
                        py, pyb = cps_next()
                        yield from g_mm(pg[:], pgb, [(sg[:, kc, ws], u[:, kc, :]) for kc in range(KC)], [sgb, ub])
                        yield from g_mm(py[:], pyb, [(spp[:, k2, ct * 128:(ct + 1) * 128], pth[:, k2, :]) for k2 in range(2)],
                                        [sppb, pthb])
                        tt, ttb = tmp_next()
                        g_gate(tt, ttb, pg, pgb)
                        yield
                        op(DVE, lambda tt=tt, py=py: nc.vector.tensor_tensor(
                            out=tt[:], in0=py[:], in1=tt[:], op=ALU.mult), [ttb, pyb], [ttb])
                        op(DVE, lambda tt=tt, ct=ct: nc.vector.tensor_tensor(
                            out=hb[:, ct, :], in0=tt[:], in1=hb[:, ct, :], op=ALU.add), [ttb, hbb], [hbb])
                        yield
                yield ("pause", 2)
                yield from g_rmsnorm(hb, hbb, 4, pp, bigb)
                dma(POOL, bigb.dsem, osrc[:, :, cs_i], pp[:, :, 0:512], reads=[bigb])
                yield

            C_MM_EST = 1200.0
            D1, D2 = 2, 6
            NZS, NSP, NDD, NAA = 4, 5, 2, 6

            def chain_ih(c):
                return c // H, c % H

            def load_kv(c):
                i_, h_ = chain_ih(c)
                r_ = c % 2
                L = (2 * i_ + 2) * 512
                dma(POOL, kh[r_][1].dsem, kh[r_][0][:, 0:L], kT_s[h_, :, 0:L], writes=[kh[r_][1]])
                dma(POOL, vh[r_][1].dsem, vh[r_][0][:, 0:L // 128, :], v_s[:, 0:L // 128, h_ * 128:(h_ + 1) * 128],
                    writes=[vh[r_][1]])

            NCHAIN = NO * H
            load_kv(0)
            if NCHAIN > 1:
                load_kv(1)
            G_total = sum(H * (8 * i_ + 8) + D2 for i_ in range(NO))
            C_TOTAL = C_MM_EST * (NO - 1) + 400.0
            cst8 = {"j": 0, "gen": None, "pause": 0, "credit": 0.0, "g": 0}

            def c_advance(avail):
                cst8["g"] += 1
                done = cmm[0]
                cst8["credit"] += max(0.0, C_TOTAL - done) / max(1, G_total - cst8["g"] + 1)
                if cst8["pause"] > 0:
                    cst8["pause"] -= 1
                    return
                burst = 0
                mm0 = cmm[0]
                dve0 = DVE.cnt
                while cst8["credit"] > 0 and burst < 6 and cmm[0] - mm0 < 8 and DVE.cnt - dve0 < 2:
                    if cst8.get("need") is not None:
                        if cst8["need"] >= avail:
                            return
                        cst8["need"] = None
                    if cst8["gen"] is None:
                        if cst8["j"] > avail or cst8["j"] >= NO:
                            return
                        cst8["gen"] = c_iter(cst8["j"])
                    before = cmm[0]
                    try:
                        rv = next(cst8["gen"])
                    except StopIteration:
                        cst8["gen"] = None
                        cst8["j"] += 1
                        continue
                    cst8["credit"] -= (cmm[0] - before)
                    burst += 1
                    if isinstance(rv, tuple):
                        if rv[0] == "need_o":
                            cst8["need"] = rv[1]
                            continue
                        cst8["pause"] = rv[1]
                        return

            for i in range(NO):
                dma(SP, qcb.dsem, qc[:], qT_s[:, :, i * 512:(i + 1) * 512].rearrange("h d s -> d h s"), writes=[qcb])
                top = (2 * i + 1) * 4
                tiles = []
                for gk in range(top + 3, -1, -1):
                    c0 = (gk - top) * 128 if gk >= top else 0
                    tiles.append((gk, c0, gk >= top))
                n = len(tiles)
                flat = [(h, t) for h in range(H) for t in range(n)]
                NF = len(flat)
                def emit_z(si):
                    h, t = flat[si]
                    gk, c0, dg = tiles[t]
                    kt, ktb = kh[(i * H + h) % 2]
                    Zp, Zpb = Zb[si % 2]
                    fz = [lambda: nc.tensor.matmul(
                        Zp[:, c0:512], lhsT=kt[:, gk * 128:(gk + 1) * 128], rhs=qc[:, h, c0:512],
                        start=True, stop=not dg, skip_group_check=True)]
                    if dg:
                        fz.append(lambda: nc.tensor.matmul(
                            Zp[:, c0:c0 + 128], lhsT=ident_h, rhs=negm_h, start=False, stop=True,
                            skip_group_check=True))
                    op_group(PE, fz, [ktb, qcb, cst_hb], [Zpb])

                emit_z(0)
                for s_i in range(NF + D2):
                    if s_i < NF:
                        h, t = flat[s_i]
                        gk, c0, dg = tiles[t]
                        Zp, Zpb = Zb[s_i % 2]
                        z, zb_ = zs[s_i % NZS]
                        sp_, sp_b = spb[s_i % NSP]
                        op(ACT, lambda: nc.scalar.activation(out=z[:, c0:512], in_=Zp[:, c0:512], func=AF.Exp),
                           [Zpb], [zb_])
                        ln_args = (z, zb_, sp_, sp_b, c0)
                    else:
                        ln_args = None
                    if D1 <= s_i < NF + D1:
                        j = s_i - D1
                        h, t = flat[j]
                        gk, c0, dg = tiles[t]
                        z, zb_ = zs[j % NZS]
                        sp_, sp_b = spb[j % NSP]
                        d_, d_b = dd[j % NDD]
                        a_, a_b = aa[j % NAA]
                        fns = []
                        rds = [cst_hb, sp_b]
                        if t > 0:
                            pgk, pc0, pdg = tiles[t - 1]
                            ps_, ps_b = spb[(j - 1) % NSP]
                            rds.append(ps_b)
                            fns.append(lambda ps_=ps_, pc0=pc0: nc.tensor.matmul(
                                Xp[:, pc0:512], lhsT=M_lt, rhs=ps_[:, pc0:512], start=False, stop=False,
                                skip_group_check=True))
                        fns.append(lambda: nc.tensor.matmul(
                            Xp[:, c0:512], lhsT=M_incl, rhs=sp_[:, c0:512], start=(t == 0), stop=False,
                            skip_group_check=True))
                        op_group(PE, fns, rds, [Xb])
                        op(ACT, lambda: nc.scalar.activation(out=d_[:, c0:512], in_=Xp[:, c0:512], func=AF.Exp,
                                                             scale=-1.0), [Xb], [d_b])
                        op(DVE, lambda: nc.vector.tensor_tensor(
                            out=a_[:, c0:512], in0=z[:, c0:512], in1=d_[:, c0:512], op=ALU.mult), [zb_, d_b], [a_b])
                    if ln_args is not None:
                        lz, lzb, lsp, lspb, lc0 = ln_args
                        op(ACT, lambda: nc.scalar.activation(out=lsp[:, lc0:512], in_=lz[:, lc0:512], func=AF.Ln,
                                                             bias=oneT[:], scale=1.0), [lzb, one_b], [lspb])
                    if s_i >= D2:
                        j = s_i - D2
                        h, t = flat[j]
                        gk, c0, dg = tiles[t]
                        vt, vtb = vh[(i * H + h) % 2]
                        a_, a_b = aa[j % NAA]
                        op(PE, lambda: nc.tensor.matmul(
                            Op[:, c0:512], lhsT=vt[:, gk, :], rhs=a_[:, c0:512], start=(t == 0), stop=(t == n - 1),
                            skip_group_check=True), [vtb, a_b], [Opb])
                        if t == n - 1:
                            o_, o_b = ost[h % 2]
                            op(DVE, lambda: nc.vector.tensor_copy(out=o_[:], in_=Op[:]), [Opb], [o_b])
                            ev = dma(POOL, o_b.dsem, oT_s[:, h, i * 512:(i + 1) * 512], o_[:], reads=[o_b])
                            o_events[i].append(ev)
                            cnext = i * H + h + 2
                            if cnext < NCHAIN:
                                load_kv(cnext)
                    if s_i + 1 < NF:
                        emit_z(s_i + 1)
                    c_advance(i)
            while cst8["j"] < NO:
                if cst8["gen"] is None:
                    cst8["gen"] = c_iter(cst8["j"])
                for _ in cst8["gen"]:
                    pass
                cst8["gen"] = None
                cst8["j"] += 1
            barrier()

    return nc


def _consts():
    sp = np.arange(128)[:, None]
    s = np.arange(128)[None, :]
    m_incl = (sp >= s).astype(np.float32)
    m_lt = (sp < s).astype(np.float32)
    mask = (sp < s).astype(np.float32)
    negb = (mask - 1.0) * (-NEGBIG)
    ident = np.eye(128, dtype=np.float32)
    return np.ascontiguousarray(np.concatenate([m_incl, m_lt, mask, negb, ident], axis=1), dtype=np.float32)


_CACHE = {}


def kernel(x, p, ffn1_norm, ffn1_w_in, ffn1_w_out, mix_norm, w_mix_in, conv_w, w_conv_out, w_attn_out,
           w_mix_out, ffn2_norm, ffn2_w_in, ffn2_w_out, ple_norm, w_ple_gate, w_ple_proj, final_norm):
    x = np.asarray(x, dtype=np.float32)
    p = np.asarray(p, dtype=np.float32)
    B, S, _ = x.shape
    assert B == 4 and S % 1024 == 0
    NCH = S // 512
    NO = NCH // 2
    if S not in _CACHE:
        _CACHE[S] = build(S)
    nc = _CACHE[S]

    def f(a):
        return np.ascontiguousarray(np.asarray(a, dtype=np.float32))

    vec_list = [ffn1_norm[0], mix_norm[0], ffn2_norm[0], ple_norm[0], final_norm, conv_w[0][0], conv_w[0][1], conv_w[0][2]]
    vecs = np.stack([np.asarray(v, dtype=np.float32).reshape(8, 128).T for v in vec_list], axis=1)
    vecs = np.ascontiguousarray(vecs.reshape(128, 64))
    shared = {
        "w_in1": f(ffn1_w_in[0]), "w_out1": f(ffn1_w_out[0]), "w_mix": f(w_mix_in[0]), "w_co": f(w_conv_out[0]),
        "w_ao": f(w_attn_out[0]), "w_mo": f(w_mix_out[0]), "w_in2": f(ffn2_w_in[0]), "w_out2": f(ffn2_w_out[0]),
        "w_pg": f(w_ple_gate[0]), "w_pp": f(w_ple_proj[0]), "vecs": vecs, "consts": _consts(),
    }
    in_maps = []
    for c in range(8):
        b, pi = c // 2, c % 2
        xs = np.zeros((S, D), dtype=np.float32)
        if pi == 1:
            xs[:] = x[b]
        else:
            xs[512:] = x[b, : S - 512]
        own = np.concatenate([p[0, b, (2 * i + pi) * 512:(2 * i + pi + 1) * 512] for i in range(NO)], axis=0)
        m = dict(shared)
        m["xT"] = np.ascontiguousarray(xs.T)
        m["pT"] = np.ascontiguousarray(own.T)
        in_maps.append(m)
    res = run_bass_kernel_spmd(nc, in_maps, core_ids=list(range(8)))
    out = np.empty((B, S, D), dtype=np.float32)
    for c in range(8):
        b, pi = c // 2, c % 2
        oT = np.asarray(res.results[c]["outT"])
        for i in range(NO):
            out[b, (2 * i + pi) * 512:(2 * i + pi + 1) * 512] = oT[:, i * 512:(i + 1) * 512].T
    return out
```

```python
import math
from contextlib import ExitStack

import numpy as np
import concourse.bass as bass
import concourse.mybir as mybir
from concourse.bass_utils import run_bass_kernel_spmd

F32 = mybir.dt.float32
BF16 = mybir.dt.bfloat16
AF = mybir.ActivationFunctionType
ALU = mybir.AluOpType

D = 1024
H = 8
DFF = 2816
PLE = 256
KC = 8
FC = 22
EPS = 1e-6
NEGBIG = -30000.0
RING = 3
SLOT_ELEMS = 4096


class Buf:
    __slots__ = ("name", "w", "r", "dsem")

    def __init__(self, name):
        self.name = name
        self.w = None
        self.r = {}
        self.dsem = None


class DSem:
    __slots__ = ("sem", "cnt")

    def __init__(self, sem):
        self.sem = sem
        self.cnt = 0


class Eng:
    def __init__(self, handle, sem, selfwait):
        self.h = handle
        self.sem = sem
        self.cnt = 0
        self.waited = {}
        self.selfwait = selfwait

    def wait(self, ev):
        sem, val = ev
        if sem is self.sem and not self.selfwait:
            return
        if self.waited.get(sem, 0) >= val:
            return
        self.h.wait_ge(sem, val)
        self.waited[sem] = val


def _deps(reads, writes):
    deps = {}
    for b in reads:
        if b.w is not None:
            s, v = b.w
            if deps.get(s, 0) < v:
                deps[s] = v
    for b in writes:
        if b.w is not None:
            s, v = b.w
            if deps.get(s, 0) < v:
                deps[s] = v
        for s, v in b.r.items():
            if deps.get(s, 0) < v:
                deps[s] = v
    return deps


def _record(ev, reads, writes):
    s, v = ev
    for b in reads:
        if b.r.get(s, 0) < v:
            b.r[s] = v
    for b in writes:
        b.w = ev
        b.r = {}


def op(eng, fn, reads=(), writes=()):
    for s, v in _deps(reads, writes).items():
        eng.wait((s, v))
    ins = fn()
    ins.then_inc(eng.sem, 1)
    eng.cnt += 1
    ev = (eng.sem, eng.cnt)
    _record(ev, reads, writes)
    return ev


def op_group(eng, fns, reads=(), writes=()):
    for s, v in _deps(reads, writes).items():
        eng.wait((s, v))
    ins = None
    for fn in fns:
        ins = fn()
    ins.then_inc(eng.sem, 1)
    eng.cnt += 1
    ev = (eng.sem, eng.cnt)
    _record(ev, reads, writes)
    return ev


def dma(q, ds, out_ap, in_ap, reads=(), writes=(), extra=()):
    deps = _deps(reads, writes)
    for s_, v_ in extra:
        if deps.get(s_, 0) < v_:
            deps[s_] = v_
    if ds.cnt:
        if deps.get(ds.sem, 0) < ds.cnt:
            deps[ds.sem] = ds.cnt
    for s, v in deps.items():
        q.wait((s, v))
    q.h.dma_start(out=out_ap, in_=in_ap).then_inc(ds.sem, 16)
    ds.cnt += 16
    ev = (ds.sem, ds.cnt)
    _record(ev, reads, writes)
    return ev


def build(S):
    NCH = S // 512
    NO = NCH // 2
    NB = S // 128
    SO = S // 2
    nc = bass.Bass("TRN2", target_bir_lowering=False)

    def din(name, shape, dt=F32):
        return nc.dram_tensor(name, list(shape), dt, kind="ExternalInput").ap()

    xT = din("xT", [D, S])
    pT = din("pT", [PLE, SO])
    W = {
        "in1": din("w_in1", [D, 2 * DFF]),
        "out1": din("w_out1", [DFF, D]),
        "mix": din("w_mix", [D, 8 * D]),
        "co": din("w_co", [D, D]),
        "ao": din("w_ao", [D, D]),
        "mo": din("w_mo", [D, D]),
        "in2": din("w_in2", [D, 2 * DFF]),
        "out2": din("w_out2", [DFF, D]),
        "pg": din("w_pg", [D, D]),
        "pp": din("w_pp", [PLE, D]),
    }
    vecs_d = din("vecs", [128, 64])
    consts_d = din("consts", [128, 640])
    outT = nc.dram_tensor("outT", [D, SO], F32, kind="ExternalOutput").ap()

    kT_s = nc.dram_tensor("kT_s", [H, 128, S], BF16).ap()
    qT_s = nc.dram_tensor("qT_s", [H, 128, SO], BF16).ap()
    v_s = nc.dram_tensor("v_s", [128, NB, D], BF16).ap()
    h1_s = nc.dram_tensor("h1_s", [128, KC, SO], F32).ap()
    oT_s = nc.dram_tensor("oT_s", [128, H, SO], BF16).ap()

    slabs = {}

    def add_slabs(prefix, w_ap, col0, ncols, cw, nm):
        K = w_ap.shape[0]
        kc = K // 128
        ns = ncols // cw
        wb = nc.dram_tensor("wb_" + nm, [ns, 128, kc, cw], BF16).ap()
        src = w_ap.rearrange("(k p) n -> p k n", p=128)
        for j in range(ns):
            slabs[(prefix, j)] = (src[:, :, col0 + j * cw: col0 + (j + 1) * cw], wb[j], kc, cw)

    add_slabs("in1g", W["in1"], 0, DFF, 256, "in1g")
    add_slabs("in1u", W["in1"], DFF, DFF, 256, "in1u")
    add_slabs("out1", W["out1"], 0, D, 128, "out1")
    add_slabs("mix", W["mix"], 0, 8 * D, 512, "mix")
    add_slabs("co", W["co"], 0, D, 512, "co")
    add_slabs("ao", W["ao"], 0, D, 512, "ao")
    add_slabs("mo", W["mo"], 0, D, 512, "mo")
    add_slabs("in2g", W["in2"], 0, DFF, 256, "in2g")
    add_slabs("in2u", W["in2"], DFF, DFF, 256, "in2u")
    add_slabs("out2", W["out2"], 0, D, 128, "out2")
    add_slabs("pg", W["pg"], 0, D, 512, "pg")
    add_slabs("pp", W["pp"], 0, D, 1024, "pp")

    plan = []
    for i in range(NO):
        for j in range(11):
            plan += [("in1g", j), ("in1u", j)]
        plan += [("out1", dt) for dt in range(8)]
        plan += [("mix", s) for s in (6, 7, 8, 9, 10, 11)]
    for i in range(NO):
        plan += [("mix", 2), ("mix", 4), ("mix", 3), ("mix", 5)]
        plan += [("mix", 0), ("mix", 1)]
        plan += [("mix", 12), ("co", 0), ("mix", 13), ("co", 1)]
        plan += [("mix", 14), ("ao", 0), ("mix", 15), ("ao", 1)]
        plan += [("mo", 0), ("mo", 1)]
        for j in range(11):
            plan += [("in2g", j), ("in2u", j)]
        plan += [("out2", dt) for dt in range(8)]
        plan += [("pp", 0), ("pg", 0), ("pg", 1)]

    with ExitStack() as g:
        def sem(name):
            return g.enter_context(nc.semaphore(name))

        PE = Eng(nc.tensor, sem("s_pe"), False)
        ACT = Eng(nc.scalar, sem("s_act"), True)
        DVE = Eng(nc.vector, sem("s_dve"), True)
        POOL = Eng(nc.gpsimd, sem("s_pool"), True)
        SP = Eng(nc.sync, sem("s_sp"), False)
        engines = [PE, ACT, DVE, POOL, SP]
        dsems = []

        def new_dsem(name):
            ds = DSem(sem(name))
            dsems.append(ds)
            return ds

        def barrier():
            for e in engines:
                for f in engines:
                    if f is not e and f.cnt:
                        e.wait((f.sem, f.cnt))
                for ds in dsems:
                    if ds.cnt:
                        e.wait((ds.sem, ds.cnt))

        def sb(es, name, shape, dt, with_dsem=False):
            t = es.enter_context(nc.sbuf_tensor("sb_" + name, list(shape), dt))
            b = Buf(name)
            if with_dsem:
                b.dsem = new_dsem("d_" + name)
            return t, b

        vecs, vecs_b = sb(g, "vecs", [128, 64], F32, True)
        cst_f, cst_fb = sb(g, "cst_f", [128, 640], F32, True)
        cst_h, cst_hb = sb(g, "cst_h", [128, 640], BF16)
        epsT, eps_b = sb(g, "epsT", [128, 1], F32)
        halo, halo_b = sb(g, "halo", [128, NO, KC, 2], BF16)
        ring_t, ring_b = [], []
        for r in range(RING):
            t, b = sb(g, f"ring{r}", [128, SLOT_ELEMS], BF16, True)
            ring_t.append(t)
            ring_b.append(b)
        PS, PSb = [], []
        for r in range(8):
            t = g.enter_context(nc.psum_tensor(f"ps{r}", [128, 512], F32))
            PS.append(t)
            PSb.append(Buf(f"ps{r}"))

        dma(SP, vecs_b.dsem, vecs[:], vecs_d, writes=[vecs_b])
        dma(SP, cst_fb.dsem, cst_f[:], consts_d, writes=[cst_fb])
        op(DVE, lambda: nc.vector.tensor_copy(out=cst_h[:, 0:256], in_=cst_f[:, 0:256]), [cst_fb], [cst_hb])
        op(DVE, lambda: nc.vector.memset(cst_h[:, 256:384], 1.0), [], [cst_hb])
        op(DVE, lambda: nc.vector.tensor_copy(out=cst_h[:, 384:640], in_=cst_f[:, 384:640]), [cst_fb], [cst_hb])
        op(DVE, lambda: nc.vector.memset(epsT[:], EPS), [], [eps_b])
        oneT, one_b = sb(g, "oneT", [128, 1], F32)
        op(DVE, lambda: nc.vector.memset(oneT[:], 1.0), [], [one_b])
        M_incl = cst_h[:, 0:128]
        M_lt = cst_h[:, 128:256]
        ones_h = cst_h[:, 256:384]
        negm_h = cst_h[:, 384:512]
        ident_h = cst_h[:, 512:640]
        mask_f = cst_f[:, 256:384]
        negb_f = cst_f[:, 384:512]

        def vcol(v, kc):
            return vecs[:, v * 8 + kc: v * 8 + kc + 1]

        cast_ds = [new_dsem(f"d_cast{r}") for r in range(4)]
        slab_buf = {}
        order = []
        for k in plan:
            if k not in slab_buf:
                slab_buf[k] = Buf("slab")
                order.append(k)
        for n, k in enumerate(order):
            src, dst, kc, cw = slabs[k]
            dma(POOL, cast_ds[n % 4], dst, src, writes=[slab_buf[k]])

        ring_state = {"idx": 0, "issued": 0}

        def ring_next(key, hold=0):
            idx = ring_state["idx"]
            assert plan[idx] == key, (idx, plan[idx], key)
            lim = min(len(plan), idx - hold + RING)
            while ring_state["issued"] < lim:
                j = ring_state["issued"]
                src, dst, kc, cw = slabs[plan[j]]
                sl = j % RING
                view = ring_t[sl][:, 0:kc * cw].rearrange("p (k c) -> p k c", k=kc)
                dma(SP, ring_b[sl].dsem, view, dst, reads=[slab_buf[plan[j]]], writes=[ring_b[sl]])
                ring_state["issued"] += 1
            assert ring_state["issued"] > idx
            ring_state["idx"] += 1
            src, dst, kc, cw = slabs[key]
            sl = idx % RING
            return ring_t[sl][:, 0:kc * cw].rearrange("p (k c) -> p k c", k=kc), ring_b[sl]

        ps_state = {"n": 0}

        def ps_next(lo=0, hi=8):
            n = ps_state["n"]
            ps_state["n"] += 1
            r = lo + n % (hi - lo)
            return PS[r], PSb[r]

        def mm_group(ps, psb, items, reads):
            n = len(items)
            fns = []
            for t, (l, r_) in enumerate(items):
                fns.append(lambda l=l, r_=r_, t=t: nc.tensor.matmul(
                    ps, lhsT=l, rhs=r_, start=(t == 0), stop=(t == n - 1), skip_group_check=True))
            return op_group(PE, fns, reads, [psb])

        def rmsnorm(x, xb, v, u, ub, Wd, tmps, rstd, rstd_b):
            sq, sqb = tmps["sq"]
            for kc in range(KC):
                op(ACT, lambda kc=kc: nc.scalar.activation(out=sq[:, kc, 0:Wd], in_=x[:, kc, 0:Wd], func=AF.Square),
                   [xb], [sqb])
            for hf in range(Wd // 512):
                ps, psb = ps_next()
                cs = slice(hf * 512, (hf + 1) * 512)
                mm_group(ps[:], psb, [(ones_h, sq[:, kc, cs]) for kc in range(KC)], [cst_hb, sqb])
                op(ACT, lambda ps=ps, cs=cs: nc.scalar.activation(out=rstd[:, cs], in_=ps[:], func=AF.Sqrt,
                                                                  bias=epsT[:], scale=1.0 / D),
                   [psb, eps_b], [rstd_b])
                op(DVE, lambda cs=cs: nc.vector.reciprocal(out=rstd[:, cs], in_=rstd[:, cs]), [rstd_b], [rstd_b])
            for kc in range(KC):
                op(DVE, lambda kc=kc: nc.vector.scalar_tensor_tensor(
                    out=u[:, kc, 0:Wd], in0=x[:, kc, 0:Wd], scalar=vcol(v, kc), in1=rstd[:, 0:Wd],
                    op0=ALU.mult, op1=ALU.mult), [xb, vecs_b, rstd_b], [ub])

        def ffn(u, ub, x, xb, hid, hidb, tmpl, kin_g, kin_u, kout, Wd):
            nh = Wd // 512
            tcount = 0
            for j in range(11):
                sg, sgb = ring_next((kin_g, j), 0)
                su, sub = ring_next((kin_u, j), 1)
                for ff in range(2):
                    f = 2 * j + ff
                    fs = slice(ff * 128, (ff + 1) * 128)
                    for hf in range(nh):
                        cs = slice(hf * 512, (hf + 1) * 512)
                        pg, pgb = ps_next()
                        pu, pub = ps_next()
                        mm_group(pg[:], pgb, [(sg[:, kc, fs], u[:, kc, cs]) for kc in range(KC)], [sgb, ub])
                        mm_group(pu[:], pub, [(su[:, kc, fs], u[:, kc, cs]) for kc in range(KC)], [sub, ub])
                        tt, ttb = tmpl[tcount % len(tmpl)]
                        tcount += 1
                        op(ACT, lambda tt=tt, pg=pg: nc.scalar.activation(out=tt[:], in_=pg[:], func=AF.Silu),
                           [pgb], [ttb])
                        op(DVE, lambda tt=tt, pu=pu, f=f, cs=cs: nc.vector.tensor_tensor(
                            out=hid[:, f, cs], in0=tt[:], in1=pu[:], op=ALU.mult), [ttb, pub], [hidb])
            for dt in range(8):
                so, sob = ring_next((kout, dt), 0)
                for hf in range(nh):
                    cs = slice(hf * 512, (hf + 1) * 512)
                    po, pob = ps_next()
                    mm_group(po[:], pob, [(so[:, f, :], hid[:, f, cs]) for f in range(FC)], [sob, hidb])
                    op(DVE, lambda po=po, dt=dt, cs=cs: nc.vector.scalar_tensor_tensor(
                        out=x[:, dt, cs], in0=po[:], scalar=0.5, in1=x[:, dt, cs], op0=ALU.mult, op1=ALU.add),
                       [pob, xb], [xb])

        with ExitStack() as a:
            xt, xtb = sb(a, "xt", [128, KC, 1024], F32, True)
            u, ub = sb(a, "uA", [128, KC, 1024], BF16)
            hid, hidb = sb(a, "hidA", [128, FC, 1024], BF16)
            rstd, rstd_b = sb(a, "rstdA", [128, 1024], F32)
            tmpl = [sb(a, f"tmpA{r}", [128, 512], F32) for r in range(4)]
            stg = [sb(a, f"stgA{r}", [128, 4096], BF16, True) for r in range(3)]
            stg_n = [0]

            def stg_next():
                r = stg_n[0] % 3
                stg_n[0] += 1
                return stg[r]

            tmps = {"sq": (u, ub)}
            h1st_ds = new_dsem("d_h1st")
            xsrc = xT.rearrange("(k p) s -> p k s", p=128)
            for i in range(NO):
                t0 = i * 1024
                dma(SP, xtb.dsem, xt[:], xsrc[:, :, t0:t0 + 1024], writes=[xtb])
                rmsnorm(xt, xtb, 0, u, ub, 1024, tmps, rstd, rstd_b)
                ffn(u, ub, xt, xtb, hid, hidb, tmpl, "in1g", "in1u", "out1", 1024)
                dma(POOL, h1st_ds, h1_s[:, :, i * 512:(i + 1) * 512], xt[:, :, 512:1024], reads=[xtb])
                rmsnorm(xt, xtb, 1, u, ub, 1024, tmps, rstd, rstd_b)
                op(DVE, lambda i=i: nc.vector.tensor_copy(out=halo[:, i, :, :], in_=u[:, :, 510:512]), [ub], [halo_b])
                qscale = 1.0 / math.sqrt(128.0)
                for sl in (6, 7):
                    s_, s_b = ring_next(("mix", sl), 0)
                    st, stb = stg_next()
                    stv = st[:, 0:2048].rearrange("p (h t) -> p h t", h=4)
                    for c in range(4):
                        ps, psb = ps_next()
                        mm_group(ps[:], psb, [(s_[:, kc, c * 128:(c + 1) * 128], u[:, kc, 512:1024]) for kc in range(KC)],
                                 [s_b, ub])
                        op(ACT, lambda ps=ps, stv=stv, c=c: nc.scalar.activation(out=stv[:, c, :], in_=ps[:], func=AF.Copy,
                                                                              scale=qscale), [psb], [stb])
                    h0 = (sl - 6) * 4
                    dma(POOL, stb.dsem, qT_s[h0:h0 + 4, :, i * 512:(i + 1) * 512].rearrange("h d s -> d h s"), stv,
                        reads=[stb])
                for sl in (8, 9):
                    s_, s_b = ring_next(("mix", sl), 0)
                    st, stb = stg_next()
                    stv = st[:, 0:4096].rearrange("p (h t) -> p h t", h=4)
                    for c in range(4):
                        for hf in range(2):
                            cs = slice(hf * 512, (hf + 1) * 512)
                            ps, psb = ps_next()
                            mm_group(ps[:], psb, [(s_[:, kc, c * 128:(c + 1) * 128], u[:, kc, cs]) for kc in range(KC)],
                                     [s_b, ub])
                            if hf == 0:
                                op(DVE, lambda ps=ps, stv=stv, c=c, cs=cs: nc.vector.tensor_copy(out=stv[:, c, cs], in_=ps[:]),
                                   [psb], [stb])
                            else:
                                op(ACT, lambda ps=ps, stv=stv, c=c, cs=cs: nc.scalar.copy(out=stv[:, c, cs], in_=ps[:]),
                                   [psb], [stb])
                    h0 = (sl - 8) * 4
                    dma(POOL, stb.dsem, kT_s[h0:h0 + 4, :, t0:t0 + 1024].rearrange("h d s -> d h s"), stv, reads=[stb])
                for sl in (10, 11):
                    s_, s_b = ring_next(("mix", sl), 0)
                    st, stb = stg_next()
                    stv = st[:, 0:4096].rearrange("p (b c) -> p b c", b=8)
                    ch = sl - 10
                    for tb in range(8):
                        ps, psb = ps_next()
                        mm_group(ps[:], psb, [(u[:, kc, tb * 128:(tb + 1) * 128], s_[:, kc, :]) for kc in range(KC)],
                                 [s_b, ub])
                        if tb % 2 == 0:
                            op(DVE, lambda ps=ps, stv=stv, tb=tb: nc.vector.tensor_copy(out=stv[:, tb, :], in_=ps[:]),
                               [psb], [stb])
                        else:
                            op(ACT, lambda ps=ps, stv=stv, tb=tb: nc.scalar.copy(out=stv[:, tb, :], in_=ps[:]),
                               [psb], [stb])
                    dma(POOL, stb.dsem, v_s[:, i * 8:(i + 1) * 8, ch * 512:(ch + 1) * 512], stv, reads=[stb])
            barrier()

        with ExitStack() as b_:
            kh = [sb(b_, f"kh{r}", [128, S], BF16, True) for r in range(2)]
            vh = [sb(b_, f"vh{r}", [128, NB, 128], BF16, True) for r in range(2)]
            qc, qcb = sb(b_, "qc", [128, H, 512], BF16, True)
            zs = [sb(b_, f"zs{r}", [128, 512], F32) for r in range(4)]
            spb = [sb(b_, f"sp{r}", [128, 512], BF16) for r in range(5)]
            dd = [sb(b_, f"dd{r}", [128, 512], F32) for r in range(2)]
            aa = [sb(b_, f"aa{r}", [128, 512], BF16) for r in range(6)]
            ost = [sb(b_, f"ost{r}", [128, 512], BF16, True) for r in range(2)]
            ob, obb = sb(b_, "obC", [128, H, 512], BF16, True)
            o_events = [[] for _ in range(NO)]
            Zb = [(PS[0], PSb[0]), (PS[1], PSb[1])]
            Xp, Xb = PS[2], PSb[2]
            Op, Opb = PS[3], PSb[3]
            hb, hbb = sb(b_, "hC", [128, KC, 512], F32, True)
            big, bigb = sb(b_, "bigC", [128, FC * 512], BF16, True)
            hid = big[:, :].rearrange("p (f t) -> p f t", f=FC)
            pp = big[:, 0:2 * KC * 514].bitcast(F32).rearrange("p (k t) -> p k t", k=KC)
            f1 = big[:, 0:2 * KC * 512].bitcast(F32).rearrange("p (k t) -> p k t", k=KC)
            u, ub = sb(b_, "uC", [128, KC, 512], BF16)
            mm_, mmb = sb(b_, "mmC", [128, KC, 512], BF16)
            rstd, rstd_b = sb(b_, "rstdC", [128, 512], F32)
            pth, pthb = sb(b_, "pthC", [128, 2, 512], BF16, True)
            tmpl = [sb(b_, f"tmpC{r}", [128, 512], F32) for r in range(3)]
            th, thb = sb(b_, "thC", [128, 4], F32)
            tn = [0]
            cmm = [0]

            def tmp_next():
                r = tn[0] % 3
                tn[0] += 1
                return tmpl[r]

            def cps_next():
                return ps_next(4, 8)

            def g_mm(ps_ap, psb, items, reads, step=4):
                for s_, v_ in _deps(reads, [psb]).items():
                    PE.wait((s_, v_))
                n_ = len(items)
                ins = None
                for t_, (l, r_) in enumerate(items):
                    ins = nc.tensor.matmul(ps_ap, lhsT=l, rhs=r_, start=(t_ == 0), stop=(t_ == n_ - 1),
                                           skip_group_check=True)
                    cmm[0] += 1
                    if t_ < n_ - 1 and (t_ + 1) % step == 0:
                        yield
                ins.then_inc(PE.sem, 1)
                PE.cnt += 1
                _record((PE.sem, PE.cnt), reads, [psb])
                yield

            def g_rmsnorm(x, xb, v, out, outb):
                for kc in range(KC):
                    op(DVE, lambda kc=kc: nc.vector.tensor_tensor(out=u[:, kc, :], in0=x[:, kc, :], in1=x[:, kc, :],
                                                                  op=ALU.mult), [xb], [ub])
                    if kc % 2 == 1:
                        yield
                yield ("pause", 1)
                ps, psb = cps_next()
                yield from g_mm(ps[:], psb, [(ones_h, u[:, kc, :]) for kc in range(KC)], [cst_hb, ub], step=8)
                yield ("pause", 1)
                op(ACT, lambda: nc.scalar.activation(out=rstd[:], in_=ps[:], func=AF.Ln, bias=epsT[:], scale=1.0 / D),
                   [psb, eps_b], [rstd_b])
                op(ACT, lambda: nc.scalar.activation(out=rstd[:], in_=rstd[:], func=AF.Exp, scale=-0.5),
                   [rstd_b], [rstd_b])
                for kc in range(KC):
                    op(DVE, lambda kc=kc: nc.vector.scalar_tensor_tensor(
                        out=out[:, kc, 0:512], in0=x[:, kc, :], scalar=vcol(v, kc), in1=rstd[:],
                        op0=ALU.mult, op1=ALU.mult), [xb, vecs_b, rstd_b], [outb])
                    if kc % 2 == 1:
                        yield
                yield ("pause", 1)

            def g_gate(tt, ttb, pg, pgb):
                op(ACT, lambda: nc.scalar.activation(out=tt[:], in_=pg[:], func=AF.Exp, scale=-1.0), [pgb], [ttb])
                op(DVE, lambda: nc.vector.tensor_scalar(out=tt[:], in0=tt[:], scalar1=1.0, scalar2=None, op0=ALU.add),
                   [ttb], [ttb])
                op(DVE, lambda: nc.vector.reciprocal(out=tt[:], in_=tt[:]), [ttb], [ttb])

            psrc = pT.rearrange("(k p) s -> p k s", p=128)
            osrc = outT.rearrange("(k p) s -> p k s", p=128)

            def c_iter(i):
                cs_i = slice(i * 512, (i + 1) * 512)
                dma(SP, hbb.dsem, hb[:], h1_s[:, :, cs_i], writes=[hbb])
                dma(POOL, pthb.dsem, pth[:], psrc[:, :, cs_i], writes=[pthb])
                yield ("pause", 4)
                yield from g_rmsnorm(hb, hbb, 1, u, ub)
                for hf in range(2):
                    scc, sccb = ring_next(("mix", 2 + hf), 0)
                    scx, scxb = ring_next(("mix", 4 + hf), 1)
                    for c in range(4):
                        ct = hf * 4 + c
                        ws = slice(c * 128, (c + 1) * 128)
                        p1, p1b = cps_next()
                        p2, p2b = cps_next()
                        ph, phb = cps_next()
                        yield from g_mm(p1[:], p1b, [(scc[:, kc, ws], u[:, kc, :]) for kc in range(KC)], [sccb, ub])
                        yield from g_mm(p2[:], p2b, [(scx[:, kc, ws], u[:, kc, :]) for kc in range(KC)], [scxb, ub])
                        fns = []
                        for kc in range(KC):
                            fns.append(lambda kc=kc, ph=ph, scc=scc, ws=ws: nc.tensor.matmul(
                                ph[:, 0:2], lhsT=scc[:, kc, ws], rhs=halo[:, i, kc, :], start=(kc == 0), stop=(kc == KC - 1),
                                skip_group_check=True))
                        for kc in range(KC):
                            fns.append(lambda kc=kc, ph=ph, scx=scx, ws=ws: nc.tensor.matmul(
                                ph[:, 2:4], lhsT=scx[:, kc, ws], rhs=halo[:, i, kc, :], start=False, stop=(kc == KC - 1),
                                skip_group_check=True))
                        op_group(PE, fns, [sccb, scxb, halo_b], [phb])
                        cmm[0] += 2
                        tt, ttb = tmp_next()
                        op(DVE, lambda tt=tt, p1=p1: nc.vector.tensor_copy(out=tt[:], in_=p1[:]), [p1b], [ttb])
                        op(DVE, lambda tt=tt, p2=p2, ct=ct: nc.vector.tensor_tensor(
                            out=pp[:, ct, 2:514], in0=tt[:], in1=p2[:], op=ALU.mult), [ttb, p2b], [bigb])
                        yield
                        op(DVE, lambda ph=ph: nc.vector.tensor_copy(out=th[:], in_=ph[:, 0:4]), [phb], [thb])
                        op(DVE, lambda ct=ct: nc.vector.tensor_tensor(
                            out=pp[:, ct, 0:2], in0=th[:, 0:2], in1=th[:, 2:4], op=ALU.mult), [thb], [bigb])
                        yield
                for hf in range(2):
                    scb, scbb = ring_next(("mix", hf), 0)
                    for c in range(4):
                        ct = hf * 4 + c
                        ws = slice(c * 128, (c + 1) * 128)
                        p1, p1b = cps_next()
                        yield from g_mm(p1[:], p1b, [(scb[:, kc, ws], u[:, kc, :]) for kc in range(KC)], [scbb, ub])
                        tt, ttb = tmp_next()
                        op(DVE, lambda tt=tt, ct=ct: nc.vector.tensor_scalar(
                            out=tt[:], in0=pp[:, ct, 0:512], scalar1=vcol(5, ct), scalar2=None, op0=ALU.mult),
                           [bigb, vecs_b], [ttb])
                        op(DVE, lambda tt=tt, ct=ct: nc.vector.scalar_tensor_tensor(
                            out=tt[:], in0=pp[:, ct, 1:513], scalar=vcol(6, ct), in1=tt[:], op0=ALU.mult, op1=ALU.add),
                           [bigb, vecs_b, ttb], [ttb])
                        yield
                        op(DVE, lambda tt=tt, ct=ct: nc.vector.scalar_tensor_tensor(
                            out=tt[:], in0=pp[:, ct, 2:514], scalar=vcol(7, ct), in1=tt[:], op0=ALU.mult, op1=ALU.add),
                           [bigb, vecs_b, ttb], [ttb])
                        op(DVE, lambda tt=tt, p1=p1, ct=ct: nc.vector.tensor_tensor(
                            out=mm_[:, ct, :], in0=tt[:], in1=p1[:], op=ALU.mult), [ttb, p1b], [mmb])
                        yield
                yield ("pause", 2)
                for hf in range(2):
                    sg, sgb = ring_next(("mix", 12 + hf), 0)
                    sw, swb = ring_next(("co", hf), 1)
                    for c in range(4):
                        ct = hf * 4 + c
                        ws = slice(c * 128, (c + 1) * 128)
                        pg, pgb = cps_next()
                        py, pyb = cps_next()
                        yield from g_mm(pg[:], pgb, [(sg[:, kc, ws], u[:, kc, :]) for kc in range(KC)], [sgb, ub])
                        yield from g_mm(py[:], pyb, [(sw[:, kc, ws], mm_[:, kc, :]) for kc in range(KC)], [swb, mmb])
                        tt, ttb = tmp_next()
                        g_gate(tt, ttb, pg, pgb)
                        yield
                        op(DVE, lambda tt=tt, py=py, ct=ct: nc.vector.tensor_tensor(
                            out=f1[:, ct, :], in0=py[:], in1=tt[:], op=ALU.mult), [ttb, pyb], [bigb])
                        yield
                yield ("need_o", i)
                dma(SP, obb.dsem, ob[:], oT_s[:, :, cs_i], writes=[obb], extra=o_events[i])
                yield ("pause", 2)
                for hf in range(2):
                    sg, sgb = ring_next(("mix", 14 + hf), 0)
                    sw, swb = ring_next(("ao", hf), 1)
                    for c in range(4):
                        ct = hf * 4 + c
                        ws = slice(c * 128, (c + 1) * 128)
                        pg, pgb = cps_next()
                        py, pyb = cps_next()
                        yield from g_mm(pg[:], pgb, [(sg[:, kc, ws], u[:, kc, :]) for kc in range(KC)], [sgb, ub])
                        yield from g_mm(py[:], pyb, [(sw[:, kc, ws], ob[:, kc, :]) for kc in range(KC)], [swb, obb])
                        tt, ttb = tmp_next()
                        g_gate(tt, ttb, pg, pgb)
                        yield
                        op(DVE, lambda tt=tt, py=py: nc.vector.tensor_tensor(
                            out=tt[:], in0=py[:], in1=tt[:], op=ALU.mult), [ttb, pyb], [ttb])
                        op(DVE, lambda tt=tt, ct=ct: nc.vector.tensor_tensor(
                            out=mm_[:, ct, :], in0=tt[:], in1=f1[:, ct, :], op=ALU.add), [ttb, bigb], [mmb])
                        yield
                yield ("pause", 2)
                for hf in range(2):
                    sw, swb = ring_next(("mo", hf), 0)
                    for c in range(4):
                        ct = hf * 4 + c
                        ws = slice(c * 128, (c + 1) * 128)
                        po, pob = cps_next()
                        yield from g_mm(po[:], pob, [(sw[:, kc, ws], mm_[:, kc, :]) for kc in range(KC)], [swb, mmb])
                        op(DVE, lambda po=po, ct=ct: nc.vector.tensor_tensor(
                            out=hb[:, ct, :], in0=po[:], in1=hb[:, ct, :], op=ALU.add), [pob, hbb], [hbb])
                        yield
                yield ("pause", 2)
                yield from g_rmsnorm(hb, hbb, 2, u, ub)
                for j in range(11):
                    sg, sgb = ring_next(("in2g", j), 0)
                    su, sub = ring_next(("in2u", j), 1)
                    for ff in range(2):
                        f = 2 * j + ff
                        fs = slice(ff * 128, (ff + 1) * 128)
                        pg, pgb = cps_next()
                        pu, pub = cps_next()
                        yield from g_mm(pg[:], pgb, [(sg[:, kc, fs], u[:, kc, :]) for kc in range(KC)], [sgb, ub])
                        yield from g_mm(pu[:], pub, [(su[:, kc, fs], u[:, kc, :]) for kc in range(KC)], [sub, ub])
                        tt, ttb = tmp_next()
                        g_gate(tt, ttb, pg, pgb)
                        yield
                        op(DVE, lambda tt=tt, pg=pg: nc.vector.tensor_tensor(
                            out=tt[:], in0=pg[:], in1=tt[:], op=ALU.mult), [ttb, pgb], [ttb])
                        op(DVE, lambda tt=tt, pu=pu, f=f: nc.vector.tensor_tensor(
                            out=hid[:, f, :], in0=tt[:], in1=pu[:], op=ALU.mult), [ttb, pub], [bigb])
                        yield
                yield ("pause", 2)
                for dt in range(8):
                    so, sob = ring_next(("out2", dt), 0)
                    po, pob = cps_next()
                    yield from g_mm(po[:], pob, [(so[:, f, :], hid[:, f, :]) for f in range(FC)], [sob, bigb])
                    op(DVE, lambda po=po, dt=dt: nc.vector.scalar_tensor_tensor(
                        out=hb[:, dt, :], in0=po[:], scalar=0.5, in1=hb[:, dt, :], op0=ALU.mult, op1=ALU.add),
                       [pob, hbb], [hbb])
                    yield
                yield ("pause", 2)
                yield from g_rmsnorm(hb, hbb, 3, u, ub)
                spp, sppb = ring_next(("pp", 0), 0)
                for hf in range(2):
                    sg, sgb = ring_next(("pg", hf), 1 + hf)
                    for c in range(4):
                        ct = hf * 4 + c
                        ws = slice(c * 128, (c + 1) * 128)
                        pg, pgb = cps_next()
                        py, pyb = cps_next()
                        yield from g_mm(pg[:], pgb, [(sg[:, kc, ws], u[:, kc, :]) for kc in range(KC)], [sgb, ub])
                        yield from g_mm(py[:], pyb, [(spp[:, k2, ct * 128:(ct + 1) * 128], pth[:, k2, :]) for k2 in range(2)],
                                        [sppb, pthb])
                        tt, ttb = tmp_next()
                        g_gate(tt, ttb, pg, pgb)
                        yield
                        op(DVE, lambda tt=tt, py=py: nc.vector.tensor_tensor(
                            out=tt[:], in0=py[:], in1=tt[:], op=ALU.mult), [ttb, pyb], [ttb])
                        op(DVE, lambda tt=tt, ct=ct: nc.vector.tensor_tensor(
                            out=hb[:, ct, :], in0=tt[:], in1=hb[:, ct, :], op=ALU.add), [ttb, hbb], [hbb])
                        yield
                yield ("pause", 2)
                yield from g_rmsnorm(hb, hbb, 4, pp, bigb)
                dma(POOL, bigb.dsem, osrc[:, :, cs_i], pp[:, :, 0:512], reads=[bigb])
                yield

            C_MM_EST = 1200.0
            D1, D2 = 2, 6
            NZS, NSP, NDD, NAA = 4, 5, 2, 6

            def chain_ih(c):
                return c // H, c % H

            def load_kv(c):
                i_, h_ = chain_ih(c)
                r_ = c % 2
                L = (2 * i_ + 2) * 512
                dma(POOL, kh[r_][1].dsem, kh[r_][0][:, 0:L], kT_s[h_, :, 0:L], writes=[kh[r_][1]])
                dma(POOL, vh[r_][1].dsem, vh[r_][0][:, 0:L // 128, :], v_s[:, 0:L // 128, h_ * 128:(h_ + 1) * 128],
                    writes=[vh[r_][1]])

            NCHAIN = NO * H
            load_kv(0)
            if NCHAIN > 1:
                load_kv(1)
            G_total = sum(H * (8 * i_ + 8) + D2 for i_ in range(NO))
            C_TOTAL = C_MM_EST * (NO - 1) + 400.0
            cst8 = {"j": 0, "gen": None, "pause": 0, "credit": 0.0, "g": 0}

            def c_advance(avail):
                cst8["g"] += 1
                done = cmm[0]
                cst8["credit"] += 1.15 * max(0.0, C_TOTAL - done) / max(1, G_total - cst8["g"] + 1)
                if cst8["pause"] > 0:
                    cst8["pause"] -= 1
                    return
                burst = 0
                mm0 = cmm[0]
                dve0 = DVE.cnt
                while cst8["credit"] > 0 and burst < 6 and cmm[0] - mm0 < 8 and DVE.cnt - dve0 < 2:
                    if cst8.get("need") is not None:
                        if cst8["need"] >= avail:
                            return
                        cst8["need"] = None
                    if cst8["gen"] is None:
                        if cst8["j"] > avail or cst8["j"] >= NO:
                            return
                        cst8["gen"] = c_iter(cst8["j"])
                    before = cmm[0]
                    try:
                        rv = next(cst8["gen"])
                    except StopIteration:
                        cst8["gen"] = None
                        cst8["j"] += 1
                        continue
                    cst8["credit"] -= (cmm[0] - before)
                    burst += 1
                    if isinstance(rv, tuple):
                        if rv[0] == "need_o":
                            cst8["need"] = rv[1]
                            continue
                        cst8["pause"] = rv[1]
                        return

            for i in range(NO):
                dma(SP, qcb.dsem, qc[:], qT_s[:, :, i * 512:(i + 1) * 512].rearrange("h d s -> d h s"), writes=[qcb])
                top = (2 * i + 1) * 4
                tiles = []
                for gk in range(top + 3, -1, -1):
                    c0 = (gk - top) * 128 if gk >= top else 0
                    tiles.append((gk, c0, gk >= top))
                n = len(tiles)
                flat = [(h, t) for h in range(H) for t in range(n)]
                NF = len(flat)
                def emit_z(si):
                    h, t = flat[si]
                    gk, c0, dg = tiles[t]
                    kt, ktb = kh[(i * H + h) % 2]
                    Zp, Zpb = Zb[si % 2]
                    fz = [lambda: nc.tensor.matmul(
                        Zp[:, c0:512], lhsT=kt[:, gk * 128:(gk + 1) * 128], rhs=qc[:, h, c0:512],
                        start=True, stop=not dg, skip_group_check=True)]
                    if dg:
                        fz.append(lambda: nc.tensor.matmul(
                            Zp[:, c0:c0 + 128], lhsT=ident_h, rhs=negm_h, start=False, stop=True,
                            skip_group_check=True))
                    op_group(PE, fz, [ktb, qcb, cst_hb], [Zpb])

                emit_z(0)
                for s_i in range(NF + D2):
                    if s_i < NF:
                        h, t = flat[s_i]
                        gk, c0, dg = tiles[t]
                        Zp, Zpb = Zb[s_i % 2]
                        z, zb_ = zs[s_i % NZS]
                        sp_, sp_b = spb[s_i % NSP]
                        op(ACT, lambda: nc.scalar.activation(out=z[:, c0:512], in_=Zp[:, c0:512], func=AF.Exp),
                           [Zpb], [zb_])
                        ln_args = (z, zb_, sp_, sp_b, c0)
                    else:
                        ln_args = None
                    if D1 <= s_i < NF + D1:
                        j = s_i - D1
                        h, t = flat[j]
                        gk, c0, dg = tiles[t]
                        z, zb_ = zs[j % NZS]
                        sp_, sp_b = spb[j % NSP]
                        d_, d_b = dd[j % NDD]
                        a_, a_b = aa[j % NAA]
                        fns = []
                        rds = [cst_hb, sp_b]
                        if t > 0:
                            pgk, pc0, pdg = tiles[t - 1]
                            ps_, ps_b = spb[(j - 1) % NSP]
                            rds.append(ps_b)
                            fns.append(lambda ps_=ps_, pc0=pc0: nc.tensor.matmul(
                                Xp[:, pc0:512], lhsT=M_lt, rhs=ps_[:, pc0:512], start=False, stop=False,
                                skip_group_check=True))
                        fns.append(lambda: nc.tensor.matmul(
                            Xp[:, c0:512], lhsT=M_incl, rhs=sp_[:, c0:512], start=(t == 0), stop=False,
                            skip_group_check=True))
                        op_group(PE, fns, rds, [Xb])
                        op(ACT, lambda: nc.scalar.activation(out=d_[:, c0:512], in_=Xp[:, c0:512], func=AF.Exp,
                                                             scale=-1.0), [Xb], [d_b])
                        op(DVE, lambda: nc.vector.tensor_tensor(
                            out=a_[:, c0:512], in0=z[:, c0:512], in1=d_[:, c0:512], op=ALU.mult), [zb_, d_b], [a_b])
                    if ln_args is not None:
                        lz, lzb, lsp, lspb, lc0 = ln_args
                        op(ACT, lambda: nc.scalar.activation(out=lsp[:, lc0:512], in_=lz[:, lc0:512], func=AF.Ln,
                                                             bias=oneT[:], scale=1.0), [lzb, one_b], [lspb])
                    if s_i >= D2:
                        j = s_i - D2
                        h, t = flat[j]
                        gk, c0, dg = tiles[t]
                        vt, vtb = vh[(i * H + h) % 2]
                        a_, a_b = aa[j % NAA]
                        op(PE, lambda: nc.tensor.matmul(
                            Op[:, c0:512], lhsT=vt[:, gk, :], rhs=a_[:, c0:512], start=(t == 0), stop=(t == n - 1),
                            skip_group_check=True), [vtb, a_b], [Opb])
                        if t == n - 1:
                            o_, o_b = ost[h % 2]
                            op(DVE, lambda: nc.vector.tensor_copy(out=o_[:], in_=Op[:]), [Opb], [o_b])
                            ev = dma(POOL, o_b.dsem, oT_s[:, h, i * 512:(i + 1) * 512], o_[:], reads=[o_b])
                            o_events[i].append(ev)
                            cnext = i * H + h + 2
                            if cnext < NCHAIN:
                                load_kv(cnext)
                    if s_i + 1 < NF:
                        emit_z(s_i + 1)
                    c_advance(i)
            while cst8["j"] < NO:
                if cst8["gen"] is None:
                    cst8["gen"] = c_iter(cst8["j"])
                for _ in cst8["gen"]:
                    pass
                cst8["gen"] = None
                cst8["j"] += 1
            barrier()

    return nc


def _consts():
    sp = np.arange(128)[:, None]
    s = np.arange(128)[None, :]
    m_incl = (sp >= s).astype(np.float32)
    m_lt = (sp < s).astype(np.float32)
    mask = (sp < s).astype(np.float32)
    negb = (mask - 1.0) * (-NEGBIG)
    ident = np.eye(128, dtype=np.float32)
    return np.ascontiguousarray(np.concatenate([m_incl, m_lt, mask, negb, ident], axis=1), dtype=np.float32)


_CACHE = {}


def kernel(x, p, ffn1_norm, ffn1_w_in, ffn1_w_out, mix_norm, w_mix_in, conv_w, w_conv_out, w_attn_out,
           w_mix_out, ffn2_norm, ffn2_w_in, ffn2_w_out, ple_norm, w_ple_gate, w_ple_proj, final_norm):
    x = np.asarray(x, dtype=np.float32)
    p = np.asarray(p, dtype=np.float32)
    B, S, _ = x.shape
    assert B == 4 and S % 1024 == 0
    NCH = S // 512
    NO = NCH // 2
    if S not in _CACHE:
        _CACHE[S] = build(S)
    nc = _CACHE[S]

    def f(a):
        return np.ascontiguousarray(np.asarray(a, dtype=np.float32))

    vec_list = [ffn1_norm[0], mix_norm[0], ffn2_norm[0], ple_norm[0], final_norm, conv_w[0][0], conv_w[0][1], conv_w[0][2]]
    vecs = np.stack([np.asarray(v, dtype=np.float32).reshape(8, 128).T for v in vec_list], axis=1)
    vecs = np.ascontiguousarray(vecs.reshape(128, 64))
    shared = {
        "w_in1": f(ffn1_w_in[0]), "w_out1": f(ffn1_w_out[0]), "w_mix": f(w_mix_in[0]), "w_co": f(w_conv_out[0]),
        "w_ao": f(w_attn_out[0]), "w_mo": f(w_mix_out[0]), "w_in2": f(ffn2_w_in[0]), "w_out2": f(ffn2_w_out[0]),
        "w_pg": f(w_ple_gate[0]), "w_pp": f(w_ple_proj[0]), "vecs": vecs, "consts": _consts(),
    }
    in_maps = []
    for c in range(8):
        b, pi = c // 2, c % 2
        xs = np.zeros((S, D), dtype=np.float32)
        if pi == 1:
            xs[:] = x[b]
        else:
            xs[512:] = x[b, : S - 512]
        own = np.concatenate([p[0, b, (2 * i + pi) * 512:(2 * i + pi + 1) * 512] for i in range(NO)], axis=0)
        m = dict(shared)
        m["xT"] = np.ascontiguousarray(xs.T)
        m["pT"] = np.ascontiguousarray(own.T)
        in_maps.append(m)
    res = run_bass_kernel_spmd(nc, in_maps, core_ids=list(range(8)))
    out = np.empty((B, S, D), dtype=np.float32)
    for c in range(8):
        b, pi = c // 2, c % 2
        oT = np.asarray(res.results[c]["outT"])
        for i in range(NO):
            out[b, (2 * i + pi) * 512:(2 * i + pi + 1) * 512] = oT[:, i * 512:(i + 1) * 512].T
    return out
```
